# Optimizing a Trainium2 kernel written in Bass

```python
import jax, jax.numpy as jnp
from jax import lax
import numpy as np

D_MODEL = 1024
BATCH = 8
SEQ = 4096
DEPTH = 1

N_MLA_HEADS = 8
MLA_Q_RANK = 256
MLA_KV_RANK = 128
MLA_NOPE_DIM = 64
MLA_ROPE_DIM = 32
MLA_V_DIM = 64
ROPE_THETA = 10000.0
Q_BLOCK = 128

N_GDN_HEADS = 8
GDN_HEAD_DIM = 64
GDN_CONV = 4
GDN_CHUNK = 64

MLA_W = N_MLA_HEADS * MLA_V_DIM
GDN_W = N_GDN_HEADS * GDN_HEAD_DIM
D_MIX = MLA_W + GDN_W
D_IN = MLA_Q_RANK + MLA_KV_RANK + MLA_ROPE_DIM + 3 * GDN_W + 2 * N_GDN_HEADS + GDN_W

D_FF = 2816
EPS = 1e-6

kernel_name = "hybrid_mla_gdn_macaron_sandwich"


def rmsnorm(x, g):
    xf = x.astype(jnp.float32)
    y = xf * lax.rsqrt(jnp.mean(xf * xf, axis=-1, keepdims=True) + EPS)
    return (y * g.astype(jnp.float32)).astype(x.dtype)


def l2norm(x):
    xf = x.astype(jnp.float32)
    return xf * lax.rsqrt(jnp.sum(xf * xf, axis=-1, keepdims=True) + EPS)


def swiglu(x, w_gate, w_up, w_down):
    return (jax.nn.silu(x @ w_gate) * (x @ w_up)) @ w_down


def rope_tables(positions):
    half = MLA_ROPE_DIM // 2
    freqs = ROPE_THETA ** (-jnp.arange(half, dtype=jnp.float32) / half)
    ang = positions.astype(jnp.float32)[..., None] * freqs
    return jnp.cos(ang), jnp.sin(ang)


def apply_rope(x, cos, sin):
    x1, x2 = jnp.split(x.astype(jnp.float32), 2, axis=-1)
    return jnp.concatenate([x1 * cos - x2 * sin, x1 * sin + x2 * cos], axis=-1).astype(x.dtype)


def mla_group(c_q, c_kv, k_rope_raw, positions, q_norm_g, w_uq, kv_norm_g, w_ukv):
    B, T, _ = c_q.shape
    H = N_MLA_HEADS
    q = (rmsnorm(c_q, q_norm_g) @ w_uq).reshape(B, T, H, MLA_NOPE_DIM + MLA_ROPE_DIM)
    q_nope, q_pe = q[..., :MLA_NOPE_DIM], q[..., MLA_NOPE_DIM:]
    kv = (rmsnorm(c_kv, kv_norm_g) @ w_ukv).reshape(B, T, H, MLA_NOPE_DIM + MLA_V_DIM)
    k_nope, v = kv[..., :MLA_NOPE_DIM], kv[..., MLA_NOPE_DIM:]
    cos, sin = rope_tables(positions)
    q_pe = apply_rope(q_pe, cos[:, :, None], sin[:, :, None])
    k_pe = apply_rope(k_rope_raw, cos, sin)
    scale = (MLA_NOPE_DIM + MLA_ROPE_DIM) ** -0.5
    nb = T // Q_BLOCK
    qn_b = q_nope.reshape(B, nb, Q_BLOCK, H, MLA_NOPE_DIM).transpose(1, 0, 2, 3, 4)
    qp_b = q_pe.reshape(B, nb, Q_BLOCK, H, MLA_ROPE_DIM).transpose(1, 0, 2, 3, 4)
    key_pos = jnp.arange(T)

    def attend(args):
        qn, qp, blk = args
        s = (jnp.einsum('bqhd,bkhd->bhqk', qn, k_nope)
             + jnp.einsum('bqhr,bkr->bhqk', qp, k_pe)).astype(jnp.float32) * scale
        q_pos = blk * Q_BLOCK + jnp.arange(Q_BLOCK)
        causal = key_pos[None, :] <= q_pos[:, None]
        s = jnp.where(causal, s, -jnp.inf)
        p = jax.nn.softmax(s, axis=-1).astype(v.dtype)
        return jnp.einsum('bhqk,bkhd->bqhd', p, v)

    o = lax.map(attend, (qn_b, qp_b, jnp.arange(nb)))
    return o.transpose(1, 0, 2, 3, 4).reshape(B, T, H * MLA_V_DIM)


def causal_conv(x, w):
    K, C = w.shape
    return lax.conv_general_dilated(
        x, w[:, None, :], window_strides=(1,), padding=[(K - 1, 0)],
        dimension_numbers=('NWC', 'WIO', 'NWC'), feature_group_count=C)


def gated_delta_rule(q, k, v, g, beta):
    out_dtype = v.dtype
    B, T, H, dk = q.shape
    dv = v.shape[-1]
    C = GDN_CHUNK
    N = T // C
    f32 = jnp.float32
    q = q.astype(f32) * dk ** -0.5
    k, v, g, beta = k.astype(f32), v.astype(f32), g.astype(f32), beta.astype(f32)

    def to_chunks(t):
        return t.reshape((B, N, C, H) + t.shape[3:]).swapaxes(2, 3)

    qc, kc, vc, gc, bc = map(to_chunks, (q, k, v, g, beta))
    gc = jnp.cumsum(gc, axis=-1)
    tril = jnp.tril(jnp.ones((C, C), dtype=bool))
    strict = jnp.tril(jnp.ones((C, C), dtype=bool), -1)
    diff = gc[..., :, None] - gc[..., None, :]
    decay = jnp.exp(jnp.where(tril, diff, -jnp.inf))
    kb = kc * bc[..., None]
    L = jnp.where(strict, jnp.einsum('bnhid,bnhjd->bnhij', kb, kc) * decay, 0.0)
    A = jnp.eye(C, dtype=f32) + L
    w = lax.linalg.triangular_solve(A, kb * jnp.exp(gc)[..., None],
                                    left_side=True, lower=True, unit_diagonal=True)
    u = lax.linalg.triangular_solve(A, vc * bc[..., None],
                                    left_side=True, lower=True, unit_diagonal=True)
    attn = jnp.einsum('bnhid,bnhjd->bnhij', qc, kc) * decay
    q_dec = qc * jnp.exp(gc)[..., None]
    k_dec = kc * jnp.exp(gc[..., -1:] - gc)[..., None]
    g_last = jnp.exp(gc[..., -1])

    def step(S, xs):
        q_d, w_i, u_i, attn_i, k_d, gl = xs
        v_new = u_i - jnp.einsum('bhcd,bhde->bhce', w_i, S)
        o = jnp.einsum('bhcd,bhde->bhce', q_d, S) + jnp.einsum('bhij,bhje->bhie', attn_i, v_new)
        S = S * gl[..., None, None] + jnp.einsum('bhcd,bhce->bhde', k_d, v_new)
        return S, o

    xs = tuple(jnp.moveaxis(t, 1, 0) for t in (q_dec, w, u, attn, k_dec, g_last))
    S0 = jnp.zeros((B, H, dk, dv), f32)
    _, o = lax.scan(step, S0, xs)
    return o.transpose(1, 0, 3, 2, 4).reshape(B, T, H, dv).astype(out_dtype)


def setup_inputs(seed: int = 0) -> dict:
    key = jax.random.key(seed)
    ks = jax.random.split(key, 32)
    f32 = jnp.float32

    def nrm(k, shape, fan_in):
        return jax.random.normal(k, shape, f32) * fan_in ** -0.5

    def gain(k, shape):
        return 1.0 + 0.02 * jax.random.normal(k, shape, f32)

    L = DEPTH
    x = jax.random.normal(ks[0], (BATCH, SEQ, D_MODEL), f32)
    offset = jax.random.randint(ks[1], (BATCH, 1), 0, 2048, dtype=jnp.int32)
    positions = offset + jnp.arange(SEQ, dtype=jnp.int32)[None, :]
    a_log = jnp.log(jax.random.uniform(ks[17], (L, N_GDN_HEADS), f32, 1.0, 16.0))
    dt = jnp.exp(jax.random.uniform(ks[18], (L, N_GDN_HEADS), f32, np.log(1e-3), np.log(1e-1)))
    dt_bias = dt + jnp.log(-jnp.expm1(-dt))
    return {
        "x": x,
        "positions": positions,
        "ffn1_pre_g": gain(ks[2], (L, D_MODEL)),
        "ffn1_w_gate": nrm(ks[3], (L, D_MODEL, D_FF), D_MODEL),
        "ffn1_w_up": nrm(ks[4], (L, D_MODEL, D_FF), D_MODEL),
        "ffn1_w_down": nrm(ks[5], (L, D_FF, D_MODEL), D_FF),
        "ffn1_post_g": gain(ks[6], (L, D_MODEL)),
        "mix_pre_g": gain(ks[7], (L, D_MODEL)),
        "w_in": nrm(ks[8], (L, D_MODEL, D_IN), D_MODEL),
        "mla_q_norm_g": gain(ks[9], (L, MLA_Q_RANK)),
        "mla_w_uq": nrm(ks[10], (L, MLA_Q_RANK, N_MLA_HEADS * (MLA_NOPE_DIM + MLA_ROPE_DIM)), MLA_Q_RANK),
        "mla_kv_norm_g": gain(ks[11], (L, MLA_KV_RANK)),
        "mla_w_ukv": nrm(ks[12], (L, MLA_KV_RANK, N_MLA_HEADS * (MLA_NOPE_DIM + MLA_V_DIM)), MLA_KV_RANK),
        "mla_out_g": gain(ks[13], (L, MLA_W)),
        "gdn_conv_w": nrm(ks[14], (L, GDN_CONV, 3 * GDN_W), GDN_CONV),
        "gdn_a_log": a_log,
        "gdn_dt_bias": dt_bias,
        "gdn_norm_g": gain(ks[15], (L, GDN_HEAD_DIM)),
        "w_out": nrm(ks[16], (L, D_MIX, D_MODEL), D_MIX),
        "mix_post_g": gain(ks[19], (L, D_MODEL)),
        "ffn2_pre_g": gain(ks[20], (L, D_MODEL)),
        "ffn2_w_gate": nrm(ks[21], (L, D_MODEL, D_FF), D_MODEL),
        "ffn2_w_up": nrm(ks[22], (L, D_MODEL, D_FF), D_MODEL),
        "ffn2_w_down": nrm(ks[23], (L, D_FF, D_MODEL), D_FF),
        "ffn2_post_g": gain(ks[24], (L, D_MODEL)),
    }


def reference(x, positions, ffn1_pre_g, ffn1_w_gate, ffn1_w_up, ffn1_w_down, ffn1_post_g,
              mix_pre_g, w_in, mla_q_norm_g, mla_w_uq, mla_kv_norm_g, mla_w_ukv, mla_out_g,
              gdn_conv_w, gdn_a_log, gdn_dt_bias, gdn_norm_g, w_out, mix_post_g,
              ffn2_pre_g, ffn2_w_gate, ffn2_w_up, ffn2_w_down, ffn2_post_g):
    B, T, _ = x.shape
    H, dh = N_GDN_HEADS, GDN_HEAD_DIM
    sizes = (MLA_Q_RANK, MLA_KV_RANK, MLA_ROPE_DIM, 3 * GDN_W, N_GDN_HEADS, N_GDN_HEADS, GDN_W)
    cuts = []
    acc = 0
    for s in sizes[:-1]:
        acc += s
        cuts.append(acc)

    for l in range(DEPTH):
        h = swiglu(rmsnorm(x, ffn1_pre_g[l]), ffn1_w_gate[l], ffn1_w_up[l], ffn1_w_down[l])
        x = x + 0.5 * rmsnorm(h, ffn1_post_g[l])

        hn = rmsnorm(x, mix_pre_g[l])
        proj = hn @ w_in[l]
        c_q, c_kv, k_pe_raw, qkv, a, b, gate = jnp.split(proj, cuts, axis=-1)

        mla_o = mla_group(c_q, c_kv, k_pe_raw, positions,
                          mla_q_norm_g[l], mla_w_uq[l], mla_kv_norm_g[l], mla_w_ukv[l])
        mla_o = rmsnorm(mla_o, mla_out_g[l])

        qkv = jax.nn.silu(causal_conv(qkv, gdn_conv_w[l]))
        q, k, v = jnp.split(qkv, 3, axis=-1)
        q = l2norm(q.reshape(B, T, H, dh))
        k = l2norm(k.reshape(B, T, H, dh))
        v = v.reshape(B, T, H, dh)
        g = -jnp.exp(gdn_a_log[l].astype(jnp.float32)) * jax.nn.softplus(
            a.astype(jnp.float32) + gdn_dt_bias[l].astype(jnp.float32))
        beta = jax.nn.sigmoid(b.astype(jnp.float32))
        o = gated_delta_rule(q, k, v, g, beta)
        o = rmsnorm(o, gdn_norm_g[l]) * jax.nn.silu(gate.reshape(B, T, H, dh))
        gdn_o = o.reshape(B, T, GDN_W)

        mixed = jnp.concatenate([mla_o, gdn_o], axis=-1) @ w_out[l]
        x = x + rmsnorm(mixed, mix_post_g[l])

        h = swiglu(rmsnorm(x, ffn2_pre_g[l]), ffn2_w_gate[l], ffn2_w_up[l], ffn2_w_down[l])
        x = x + 0.5 * rmsnorm(h, ffn2_post_g[l])
    return x
```

```python
import numpy as np
import concourse.bass as bass
import concourse.mybir as mybir
from concourse.bass_utils import run_bass_kernel_spmd

F32 = mybir.dt.float32
F32R = mybir.dt.float32r
BF16 = mybir.dt.bfloat16
I32 = mybir.dt.int32
AF = mybir.ActivationFunctionType
ALU = mybir.AluOpType
AX = mybir.AxisListType

D = 1024
DFF = 2816
NF = DFF // 128
KC = D // 128
EPS = 1e-6
TT = 256


class Buf:
    __slots__ = ("name", "w", "r", "x")

    def __init__(self, name, excl=False):
        self.name = name
        self.w = None
        self.r = {}
        self.x = excl


class Op:
    __slots__ = ("eng", "fn", "dma", "ndma", "idx", "deps", "signal", "event", "waits")


class Sched:
    def __init__(self, nc, same_engine_sync=True):
        self.nc = nc
        self.ops = []
        self.same = same_engine_sync
        self.sems = []
        self.final_bufs = []

    def new_sem(self):
        h = self.nc.alloc_semaphore(name="s%d" % len(self.sems))
        self.sems.append(h)
        return len(self.sems) - 1

    def emit(self, eng, fn, reads=(), writes=(), dma=None, ndma=1):
        op = Op()
        op.eng = eng
        op.fn = fn
        op.dma = dma
        op.ndma = ndma
        op.idx = len(self.ops)
        op.signal = False
        op.event = None
        deps = {}

        def add(d, kind):
            if d is None:
                return
            if d.dma is not None:
                need = True
            elif d.eng != eng:
                need = True
            elif eng == "pe":
                need = False
            else:
                need = self.same and kind != "war"
            if need:
                deps[d.idx] = d

        for b in reads:
            add(b.w, "raw")
            if b.x:
                for r in b.r.values():
                    if r.eng != eng:
                        add(r, "rar")
        for b in writes:
            add(b.w, "waw")
            for r in b.r.values():
                add(r, "war")
        op.deps = list(deps.values())
        for b in writes:
            b.w = op
            b.r = {}
        for b in reads:
            key = eng if dma is None else ("dma", op.idx)
            b.r[key] = op
        self.ops.append(op)
        return op

    def finalize(self):
        nc = self.nc
        ROT = 2000
        fin = self.emit("sp", None, reads=self.final_bufs)
        for op in self.ops:
            for d in op.deps:
                d.signal = True
        cnt = {}
        cur = {}
        dstate = {}
        for op in self.ops:
            if op.dma is not None:
                st = dstate.get(op.dma)
                if st is None:
                    st = [self.new_sem(), 0]
                    dstate[op.dma] = st
                st[1] += 16 * op.ndma
                op.event = (st[0], st[1])
            elif op.signal:
                e = op.eng
                if e not in cur or cnt[e] >= ROT:
                    cur[e] = self.new_sem()
                    cnt[e] = 0
                cnt[e] += 1
                op.event = (cur[e], cnt[e])
        seen = {}
        for op in self.ops:
            w = {}
            for d in op.deps:
                s, v = d.event
                if v > w.get(s, 0):
                    w[s] = v
            sn = seen.setdefault(op.eng, {})
            op.waits = [(s, v) for s, v in w.items() if v > sn.get(s, 0)]
            for s, v in op.waits:
                sn[s] = v
        sems = self.sems
        ops = self.ops

        def mk(engname):
            def run(eng):
                for op in ops:
                    if op.eng != engname:
                        continue
                    for s, v in op.waits:
                        eng.wait_ge(sems[s], v)
                    if op.fn is None:
                        continue
                    res = op.fn(eng)
                    if op.dma is not None:
                        insts = res if isinstance(res, (list, tuple)) else [res]
                        assert len(insts) == op.ndma
                        for i in insts:
                            i.then_inc(sems[op.event[0]], 16)
                    elif op.event is not None:
                        inst = res[-1] if isinstance(res, (list, tuple)) else res
                        inst.then_inc(sems[op.event[0]], 1)
            return run

        with nc.Block() as block:
            block.sync(mk("sp"))
            block.scalar(mk("act"))
            block.vector(mk("dve"))
            block.gpsimd(mk("pool"))
            block.tensor(mk("pe"))


NH = 8
W_CQ0, W_CKV0, W_KPE0, W_KPESW0, W_QKV0, W_AB0, W_GATE0, NWIN = 0, 256, 384, 480, 576, 2112, 2128, 2640
COL_G_FFN1, COL_G_MIX, COL_G_FFN2 = 0, 8, 16
COL_G_Q, COL_G_KV, COL_CONV, COL_FREQ, COL_SIGN, COL_G_OUT, COL_DTB, COL_ALOG = 24, 26, 27, 75, 76, 77, 85, 93
NCONST = 104
PIECE = 704
NSTG = 4
TWO_PI = 6.283185307179586
CW1 = 6.28125
CW2 = TWO_PI - CW1
PI_SAFE = 3.1415925


class Ctx:
    pass


class RPool:
    def __init__(self, items):
        self.items = items
        self.i = 0

    def get(self):
        it = self.items[self.i % len(self.items)]
        self.i += 1
        return it


class Region:
    def __init__(self, handle, bufs, gran_bytes, nbytes):
        self.flat = handle[:].rearrange("p a b -> p (a b)")
        self.bufs = bufs
        self.gran = gran_bytes
        self.nbytes = nbytes
        self.off = 0

    def reset(self):
        self.off = 0

    def alloc(self, free_elems, dt):
        esz = 4 if dt in (F32, I32) else 2
        nb = free_elems * esz
        self.off = (self.off + 31) // 32 * 32
        o = self.off
        assert o + nb <= self.nbytes, "region overflow"
        self.off = o + nb
        ap = self.flat[:, o // 2:(o + nb) // 2]
        if esz == 4:
            ap = ap.bitcast(dt)
        bl = self.bufs[o // self.gran:(o + nb - 1) // self.gran + 1]
        return ap, list(bl)


def wbufs(bw, kc, c0, c1):
    return [bw[kc][i] for i in range(c0 // PIECE, (c1 - 1) // PIECE + 1)]

def dma(c, out, in_, r, w, key):
    return c.S.emit("sp", (lambda e: e.dma_start(out=out, in_=in_)), reads=r, writes=w, dma=key)


def mm(c, out, lhsT, rhs, start, stop, r, w):
    return c.S.emit("pe", (lambda e: e.matmul(out, lhsT=lhsT, rhs=rhs, start=start, stop=stop)), reads=r, writes=w)


def tr(c, out, in_, ident, r, w):
    return c.S.emit("pe", (lambda e: e.transpose(out=out, in_=in_, identity=ident)), reads=r, writes=w)


def act(c, out, in_, func, r, w, **kw):
    return c.S.emit("act", (lambda e: e.activation(out=out, in_=in_, func=func, **kw)), reads=r, writes=w)


def tt(c, eng, out, in0, in1, op, r, w):
    return c.S.emit(eng, (lambda e: e.tensor_tensor(out=out, in0=in0, in1=in1, op=op)), reads=r, writes=w)


def ts(c, eng, out, in0, s1, s2, op0, op1, r, w):
    if op1 is None:
        if eng == "pool":
            return c.S.emit(eng, (lambda e: e.tensor_scalar(out=out, in0=in0, scalar1=s1, scalar2=(1.0 if op0 == ALU.mult else 0.0),
                                                            op0=op0, op1=(ALU.mult if op0 == ALU.mult else ALU.add))),
                            reads=r, writes=w)
        return c.S.emit(eng, (lambda e: e.tensor_scalar(out=out, in0=in0, scalar1=s1, scalar2=None, op0=op0)), reads=r, writes=w)
    return c.S.emit(eng, (lambda e: e.tensor_scalar(out=out, in0=in0, scalar1=s1, scalar2=s2, op0=op0, op1=op1)), reads=r, writes=w)


def stt(c, out, in0, scalar, in1, op0, op1, r, w):
    return c.S.emit("dve", (lambda e: e.scalar_tensor_tensor(out=out, in0=in0, scalar=scalar, in1=in1, op0=op0, op1=op1)),
                    reads=r, writes=w)


def cp(c, eng, out, in_, r, w):
    if eng == "act":
        return c.S.emit("act", (lambda e: e.copy(out=out, in_=in_)), reads=r, writes=w)
    return c.S.emit(eng, (lambda e: e.tensor_copy(out=out, in_=in_)), reads=r, writes=w)


def barrier(c, old, new):
    c.S.emit("pool", (lambda e: e.memset(c.dummy[:], 0.0)), reads=list(old), writes=list(old) + list(new) + [c.b_dummy])


def cast_scaled(c, out_ap, in_ap, scale_ap, reads, writes):
    i = c.cast_i
    c.cast_i += 1
    engs = c.cast_engs
    which = engs[i % len(engs)]
    if scale_ap is None:
        cp(c, which, out_ap, in_ap, reads, writes)
    elif which == "act":
        act(c, out_ap, in_ap, AF.Copy, reads, writes, scale=scale_ap)
    else:
        ts(c, which, out_ap, in_ap, scale_ap, None, ALU.mult, None, reads, writes)


def load_cast(c, dst, src, ncols, scale_ap, wb_fn, extra_reads=()):
    c0 = 0
    while c0 < ncols:
        c1 = min(ncols, c0 + PIECE)
        si = c.stg_i % NSTG
        c.stg_i += 1
        stg, bst = c.stg[si], c.b_stg[si]
        dma(c, stg[:, 0:c1 - c0], src[:, c0:c1], [], [bst], "stg%d" % si)
        cast_scaled(c, dst[:, c0:c1], stg[:, 0:c1 - c0], scale_ap, [bst, c.b_consts] + list(extra_reads), wb_fn(c0, c1))
        c0 = c1


def load_cast_piece(c, dst, src, scale_ap, wbufs_, extra_reads=()):
    si = c.stg_i % NSTG
    c.stg_i += 1
    stg, bst = c.stg[si], c.b_stg[si]
    n = dst.shape[1]
    dma(c, stg[:, 0:n], src, [], [bst], "stg%d" % si)
    cast_scaled(c, dst, stg[:, 0:n], scale_ap, [bst, c.b_consts] + list(extra_reads), wbufs_)


def ffn_load_weights(c, nm, gcol, parts=("gu", "d")):
    mats = {"g": (c.wg, c.b_wg, c.w[nm + "_g"]), "u": (c.wu, c.b_wu, c.w[nm + "_u"])}
    npc = DFF // PIECE
    for part in parts:
        if part == "d":
            wd = c.w[nm + "_d"]
            for f in range(NF):
                load_cast(c, c.wd[:, f, :], wd[f * 128:(f + 1) * 128, :], D, None, (lambda a, b, f=f: [c.b_wd[f]]))
                yield
            continue
        order = [(m, kc, p) for p in range(npc) for kc in range(KC) for m in part]
        for i, (m, kc, p) in enumerate(order):
            wsb, bufs, wd = mats[m]
            load_cast_piece(c, wsb[:, kc, p * PIECE:(p + 1) * PIECE], wd[kc * 128:(kc + 1) * 128, p * PIECE:(p + 1) * PIECE],
                            c.consts[:, gcol + kc:gcol + kc + 1], [bufs[kc][p]])
            if i % 4 == 3:
                yield


def rstd_from_ss(c, ss_ap, out_ap, n, post_scale, bst):
    p2 = float(post_scale) ** 2
    ts(c, "dve", out_ap, ss_ap, 1.0 / (n * p2), EPS / p2, ALU.mult, ALU.add, [bst], [bst])
    ncol = out_ap.shape[1]
    ex = c.cb[:, 4:5] if ncol == 1 else c.cb[:, 4:5].to_broadcast([128, ncol])
    tt(c, "pool", out_ap, out_ap, ex, ALU.pow, [bst, c.b_consts], [bst])


def load_tile(c, src, it, srcbufs=()):
    t0 = it * TT
    xi = c.xt_i % 2
    c.xt_i += 1
    xt, bxt = c.xt[xi], c.b_xt[xi]
    srcv = src[t0:t0 + TT, :].rearrange("(s p) d -> p s d", p=128)
    dma(c, xt[:], srcv, list(srcbufs), [bxt], "xt%d" % xi)
    return xt, bxt, xi


def prenorm_T(c, xt, bxt):
    for s in range(2):
        bst = c.b_st[s]
        ssap = c.st[:, s:s + 1]
        act(c, c.xn[:, s, :], xt[:, s, :], AF.Square, [bxt], [c.b_xn[s], bst], accum_out=ssap)
        rstd_from_ss(c, ssap, ssap, D, 1.0, bst)
        ts(c, "dve", c.xn[:, s, :], xt[:, s, :], ssap, None, ALU.mult, None, [bxt, bst], [c.b_xn[s]])
        xn_to_T(c, s)


def xn_to_T(c, s):
    pst, bps = c.pp.get()
    pstb = pst.bitcast(BF16)
    for kc in range(KC):
        tr(c, pstb[:, kc * 128:(kc + 1) * 128], c.xn[:, s, kc * 128:(kc + 1) * 128], c.ident_b[:], [c.b_xn[s], c.b_ident], [bps])
    cp(c, "dve", c.xnT[:, :, s * 128:(s + 1) * 128], pstb.rearrange("p (k t) -> p k t", k=KC), [bps], [c.b_xnT])


def post_residual(c, xt, bxt, ybanks):
    for s in range(2):
        bst = c.b_st[2 + s]
        for hh in range(2):
            ps, bps = ybanks[s][hh]
            col = 4 + s * 2 + hh
            act(c, c.junk[:, 0:512], ps[:], AF.Square, [bps], [c.b_junk, bst], accum_out=c.st[:, col:col + 1])
        rcol = 8 + s
        rap = c.st[:, rcol:rcol + 1]
        tt(c, "dve", rap, c.st[:, 4 + 2 * s:5 + 2 * s], c.st[:, 5 + 2 * s:6 + 2 * s], ALU.add, [bst], [bst])
        rstd_from_ss(c, rap, rap, D, c.post_scale, bst)
        for hh in range(2):
            ps, bps = ybanks[s][hh]
            ti = (s * 2 + hh) % 2
            tmp, btmp = c.tmp[ti], c.b_tmp[ti]
            gbc, b_gbc = c.gbc_use
            stt(c, tmp[:], ps[:], rap, gbc[:, hh * 512:(hh + 1) * 512], ALU.mult, ALU.mult, [bps, bst, b_gbc], [btmp])
            tt(c, "pool", xt[:, s, hh * 512:(hh + 1) * 512], xt[:, s, hh * 512:(hh + 1) * 512], tmp[:], ALU.add,
               [bxt, btmp], [bxt])


def load_gbc(c, row):
    dma(c, c.gbc[:], c.gpost_d[row:row + 1, :].partition_broadcast(128), [], [c.b_gbc], "gbc")


def ffn_pre(c, xt, bxt):
    save = c.pp
    c.pp = c.pp_gu
    prenorm_T(c, xt, bxt)
    c.pp = save


def ffn_gu(c):
    for f in range(NF):
        if c.pump is not None:
            next(c.pump, None)
            next(c.pump, None)
        ps, bps = c.pp_gu.get()
        for m, (wsb, bw) in enumerate(((c.wg, c.b_wg), (c.wu, c.b_wu))):
            for kc in range(KC):
                mm(c, ps[:, m * TT:(m + 1) * TT], wsb[:, kc, f * 128:(f + 1) * 128], c.xnT[:, kc, :],
                   kc == 0, kc == KC - 1, wbufs(bw, kc, f * 128, (f + 1) * 128) + [c.b_xnT], [bps])
        sgi = f % 2
        sg, bsg = c.sg[sgi], c.b_sg[sgi]
        act(c, sg[:], ps[:, 0:TT], AF.Silu, [bps], [bsg])
        tt(c, "dve", c.hT[:, f, :], sg[:], ps[:, TT:2 * TT], ALU.mult, [bps, bsg], [c.b_hT[f]])


def ffn_down(c, xt, bxt):
    yb = [[None, None], [None, None]]
    for s in range(2):
        for hh in range(2):
            ps, bps = c.pp_y[s * 2 + hh]
            yb[s][hh] = (ps, bps)
            for f in range(NF):
                mm(c, ps[:], c.hT[:, f, s * 128:(s + 1) * 128], c.wd[:, f, hh * 512:(hh + 1) * 512],
                   f == 0, f == NF - 1, [c.b_hT[f], c.b_wd[f]], [bps])
    c.post_scale = 0.5
    post_residual(c, xt, bxt, yb)


def ffn_core(c, xt, bxt):
    ffn_pre(c, xt, bxt)
    ffn_gu(c)
    ffn_down(c, xt, bxt)


def store_tile(c, xt, bxt, xi, dst, it, name):
    t0 = it * TT
    dstv = dst[t0:t0 + TT, :].rearrange("(s p) d -> p s d", p=128)
    bd = Buf("%s_%d" % (name, it))
    dma(c, dstv, xt[:], [bxt], [bd], "st_xt%d_%s" % (xi, name))
    return bd

def load_wo_row(c):
    kc = c.wo_next
    if kc >= KC:
        return
    c.wo_next += 1
    load_cast(c, c.wo[:, kc, :], c.w_out_d[kc * 128:(kc + 1) * 128, :], D,
              c.consts[:, COL_G_OUT + kc:COL_G_OUT + kc + 1], (lambda a, b, kc=kc: [c.b_wo[kc]]))


def ffn_phase(c, src, srcbufs, dst, name):
    T = c.T
    n = T // TT
    sb = (lambda it: [srcbufs[it]]) if srcbufs is not None else (lambda it: [])
    outb = []
    nxt = load_tile(c, src, 0, sb(0))
    ffn_pre(c, nxt[0], nxt[1])
    for it in range(n):
        xt, bxt, xi = nxt
        if it + 1 < n:
            nxt = load_tile(c, src, it + 1, sb(it + 1))
        ffn_gu(c)
        if it + 1 < n:
            ffn_pre(c, nxt[0], nxt[1])
        ffn_down(c, xt, bxt)
        outb.append(store_tile(c, xt, bxt, xi, dst, it, name))

    return outb


def build_program(T, debug=(), upto=9, p4_stop=99):
    nc = bass.Bass("TRN2", target_bir_lowering=False)
    S = Sched(nc)
    c = Ctx()
    c.nc, c.S, c.T = nc, S, T
    c.p4_stop = p4_stop
    c.p4_serial = (p4_stop == 77)
    NT = T // 128
    c.NT = NT

    def din(name, shape, dt=F32):
        return nc.dram_tensor(name, list(shape), dt, kind="ExternalInput").ap()

    def dscr(name, shape, dt):
        kind = "ExternalOutput" if name in debug else "Internal"
        return nc.dram_tensor(name, list(shape), dt, kind=kind).ap()

    c.x = din("x", [T, D])
    c.pos_d = din("pos", [1, T], I32)
    c.ident_d = din("ident", [128, 128])
    c.masks_d = din("masks", [128, 6, 128])
    c.consts_d = din("consts", [128, NCONST])
    c.gpost_d = din("gpost", [3, D])
    c.w = {}
    for nm in ("ffn1", "ffn2"):
        c.w[nm + "_g"] = din(nm + "_wg", [D, DFF])
        c.w[nm + "_u"] = din(nm + "_wu", [D, DFF])
        c.w[nm + "_d"] = din(nm + "_wd", [DFF, D])
    c.w_in_d = din("w_in_ext", [D, NWIN])
    c.w_uq_d = din("w_uq_ext", [256, 1536])
    c.w_ukv_d = din("w_ukv_ext", [128, 1024])
    c.w_out_d = din("w_out", [D, D])
    c.out = nc.dram_tensor("out", [T, D], F32, kind="ExternalOutput").ap()
    c.x1_s = dscr("x1_s", [T, D], F32)
    c.qT_s = dscr("qT_s", [NH, 96, T], BF16)
    c.knope_s = dscr("knope_s", [NH, 64, T], BF16)
    c.kpe_s = dscr("kpe_s", [32, T], BF16)
    c.v_s = dscr("v_s", [128, NT, NH * 64], BF16)
    c.gq_s = dscr("gq_s", [4, 128, T], BF16)
    c.gk_s = dscr("gk_s", [4, 128, T], BF16)
    c.kvtok_s = dscr("kvtok_s", [T, 1024], BF16)
    c.gb_s = dscr("gb_s", [T, 16], F32)
    c.gbT_s = dscr("gbT_s", [NT, 16, 128], F32)
    c.gate_s = dscr("gate_s", [T, 512], F32)
    c.x2_s = dscr("x2_s", [T, D], F32)
    c.mla_s = dscr("mla_s", [T, 512], F32)
    c.gdn_s = dscr("gdn_s", [T, 512], F32)

    def sb(name, shape, dt=F32):
        return nc.alloc_sbuf_tensor(name, list(shape), dt)

    c.ident_f = sb("ident_f", [128, 128])
    c.ident_b = sb("ident_b", [128, 128], BF16)
    c.consts = sb("consts_sb", [128, NCONST])
    c.cb = sb("cb", [128, 8])
    c.dummy = sb("dummy_t", [128, 1])
    c.gbc = sb("gbc", [128, D])
    c.wg = sb("wg", [128, KC, DFF], BF16)
    c.wu = sb("wu", [128, KC, DFF], BF16)
    c.wd = sb("wd", [128, NF, D], BF16)
    c.wo = sb("wo", [128, KC, D], BF16)
    c.stg = [sb("stg%d" % i, [128, PIECE]) for i in range(NSTG)]
    c.xt = [sb("xt%d" % i, [128, 2, D]) for i in range(2)]
    c.xn = sb("xn", [128, 2, D], BF16)
    c.xnT = sb("xnT", [128, KC, TT], BF16)
    c.hT = sb("hT", [128, NF, TT], BF16)
    c.sg = [sb("sg%d" % i, [128, TT]) for i in range(2)]
    c.tmp = [sb("tmpy%d" % i, [128, 512]) for i in range(2)]
    c.junk = sb("junk", [128, 512], BF16)
    c.st = sb("stats", [128, 16])
    c.ps = [nc.alloc_psum_tensor("ps%d" % i, [128, 512], F32) for i in range(8)]

    c.b_ident = Buf("ident")
    c.b_consts = Buf("consts")
    c.b_dummy = Buf("dummy")
    c.b_gbc = Buf("gbc")
    c.gbc_use = (c.gbc, c.b_gbc)
    c.cast_engs = ("dve", "pool", "act")
    npc = DFF // PIECE
    c.b_wg = [[Buf("wg%d_%d" % (k, h)) for h in range(npc)] for k in range(KC)]
    c.b_wu = [[Buf("wu%d_%d" % (k, h)) for h in range(npc)] for k in range(KC)]
    c.b_wd = [Buf("wd%d" % f) for f in range(NF)]
    c.b_wo = [Buf("wo%d" % k) for k in range(KC)]
    c.b_stg = [Buf("stg%d" % i) for i in range(NSTG)]
    c.b_xt = [Buf("xt%d" % i) for i in range(2)]
    c.b_xn = [Buf("xn%d" % i) for i in range(2)]
    c.b_xnT = Buf("xnT")
    c.b_hT = [Buf("hT%d" % f) for f in range(NF)]
    c.b_sg = [Buf("sg%d" % i) for i in range(2)]
    c.b_tmp = [Buf("tmp%d" % i) for i in range(2)]
    c.b_junk = Buf("junk")
    c.b_st = [Buf("st%d" % i) for i in range(16)]
    c.b_ps = [Buf("ps%d" % i, excl=True) for i in range(8)]
    c.stg_i = 0
    c.wo_next = 0
    c.cast_i = 0
    c.xt_i = 0
    allps = [(c.ps[i], c.b_ps[i]) for i in range(8)]
    c.pp = RPool(allps)
    c.pp_gu = RPool(allps[0:4])
    c.pp_y = allps[4:8]
    c.reg_u = Region(c.wu, [b for row in c.b_wu for b in row], PIECE * 2, KC * DFF * 2)
    c.reg_d = Region(c.wd, c.b_wd, D * 2, NF * D * 2)

    dma(c, c.ident_f[:], c.ident_d, [], [c.b_ident], "ident")
    dma(c, c.consts[:], c.consts_d, [], [c.b_consts], "consts")
    cp(c, "dve", c.ident_b[:], c.ident_f[:], [c.b_ident], [c.b_ident])
    for i, v in enumerate((EPS, float(np.log(0.125)), 1.0, 0.0, -0.5)):
        c.S.emit("pool", (lambda e, i=i, v=v: e.memset(c.cb[:, i:i + 1], v)), writes=[c.b_consts])

    c.pump = ffn_load_weights(c, "ffn1", COL_G_FFN1)
    for _ in range(4):
        next(c.pump)
    load_gbc(c, 0)
    c.x1_bufs = ffn_phase(c, c.x, None, c.x1_s, "x1")
    for _ in c.pump:
        pass
    c.pump = None
    c.dbg_final = list(c.x1_bufs)
    if upto >= 2:
        phase2(c)
    pump = ffn_load_weights(c, "ffn2", COL_G_FFN2, parts=("g",))
    c.cast_engs = ("dve", "pool")
    if upto >= 3:
        phase3(c, pump)
    if upto >= 4:
        c.op_on = False
        if c.op_on:
            outproj_setup(c)
        phase4(c)
    c.cast_engs = ("dve", "pool", "act")
    if upto >= 5:
        phase5(c, pump)
    S.final_bufs = S.final_bufs + c.dbg_final
    S.finalize()
    return nc


def phase2(c):
    S, T, NT = c.S, c.T, c.NT
    R = c.reg_u
    R.reset()
    R2 = c.reg_d
    R2.reset()
    allb = []

    def alloc(n, dt, reg=None):
        ap, _ = (reg or R).alloc(n, dt)
        b = Buf("p2_%d" % len(allb))
        allb.append(b)
        return ap, b

    wuq, b_wuq = alloc(2 * 1536, BF16)
    wuq = wuq.rearrange("p (k n) -> p k n", k=2)
    wukv, b_wukv = alloc(1024, BF16)
    negA, b_negA = alloc(8, F32)
    blk, b_blk = alloc(128, F32)
    mui, b_mui = alloc(128, F32, R2)
    cmb, b_cmb = alloc(2 * 16, F32, R2)
    cmb = cmb.rearrange("p (s n) -> p s n", s=2)
    gbT, b_gbT = alloc(2 * 128, F32, R2)
    gbT = gbT.rearrange("p (s n) -> p s n", s=2)
    hal = [alloc(4, F32) for _ in range(12)]
    posi, b_posi = alloc(TT, I32)
    kfi, b_kfi = alloc(2 * TT, I32)
    rA, b_rA = alloc(2 * TT, F32)
    rB, b_rB = alloc(2 * TT, F32)
    rC, b_rC = alloc(2 * TT, F32)
    tabs = [alloc(2 * TT, F32) for _ in range(2)]
    cqn, b_cqn = alloc(2 * 384, BF16)
    cqn = cqn.rearrange("p (s n) -> p s n", s=2)
    cqnT, b_cqnT = alloc(3 * TT, BF16)
    cqnT = cqnT.rearrange("p (k t) -> p k t", k=3)
    z8, b_z8 = alloc(32, F32)
    gb, b_gb = alloc(2 * 16, F32)
    gb = gb.rearrange("p (s n) -> p s n", s=2)
    gate_sb, b_gate = alloc(2 * 512, F32, R2)
    gate_sb = gate_sb.rearrange("p (s n) -> p s n", s=2)
    ktp = RPool([(alloc(TT, F32), alloc(TT, F32)) for _ in range(2)])
    kpe_rot, b_kpe = alloc(TT, BF16)
    xcp = RPool([alloc(TT + 4, F32) for _ in range(3)])
    accp = RPool([alloc(TT, F32) for _ in range(3)])
    ys = [alloc(TT, F32, R2) for _ in range(8)]
    sqp = RPool([alloc(TT, F32, R2) for _ in range(4)])
    featp = RPool([alloc(TT, BF16, R2) for _ in range(4)])
    tok, b_tok = alloc(2 * 1024, BF16, R2)
    tok = tok.rearrange("p (s n) -> p s n", s=2)
    qp = RPool([alloc(TT, BF16, R2) for _ in range(3)])
    knp = RPool([alloc(TT, BF16, R2) for _ in range(3)])
    v_sb, b_vsb = alloc(2 * 512, BF16, R2)
    v_sb = v_sb.rearrange("p (s n) -> p s n", s=2)

    barrier(c, R.bufs + R2.bufs, allb)
    for kc in range(KC):
        load_cast(c, c.wg[:, kc, 0:NWIN], c.w_in_d[kc * 128:(kc + 1) * 128, :], NWIN,
                  c.consts[:, COL_G_MIX + kc:COL_G_MIX + kc + 1], (lambda a, b, kc=kc: wbufs(c.b_wg, kc, a, b)))
    for kc in range(2):
        load_cast(c, wuq[:, kc, :], c.w_uq_d[kc * 128:(kc + 1) * 128, :], 1536,
                  c.consts[:, COL_G_Q + kc:COL_G_Q + kc + 1], (lambda a, b: [b_wuq]))
    load_cast(c, wukv, c.w_ukv_d[:, :], 1024, c.consts[:, COL_G_KV:COL_G_KV + 1], (lambda a, b: [b_wukv]))
    dma(c, blk, c.masks_d[:, 4, :], [], [b_blk], "blk")
    dma(c, mui, c.masks_d[:, 0, :], [], [b_mui], "mui")
    act(c, negA, c.consts[:, COL_ALOG:COL_ALOG + 8], AF.Exp, [c.b_consts], [b_negA])
    ts(c, "dve", negA, negA, -1.0, None, ALU.mult, None, [b_negA], [b_negA])
    for ch in range(12):
        S.emit("pool", (lambda e, ch=ch: e.memset(hal[ch][0], 0.0)), writes=[hal[ch][1]])

    gq_b, gk_b, kvtok_b, gb_b, gate_b, qT_b, kn_b, kpe_b, v_b, gbT_b = [], [], [], [], [], [], [], [], [], []
    rA3 = rA.rearrange("p (a t) -> p a t", a=2)
    def rope_args(it):
        t0 = it * TT
        dma(c, posi, c.pos_d[0:1, t0:t0 + TT].partition_broadcast(128), [], [b_posi], "posi")
        yield
        cp(c, "pool", rA[:, 0:TT], posi, [b_posi], [b_rA])
        yield
        ts(c, "pool", rA[:, 0:TT], rA[:, 0:TT], c.consts[:, COL_FREQ:COL_FREQ + 1], None, ALU.mult, None, [b_rA, c.b_consts], [b_rA])
        yield
        ts(c, "pool", rA[:, TT:2 * TT], rA[:, 0:TT], float(np.pi / 2), None, ALU.add, None, [b_rA], [b_rA])
        yield
        ts(c, "pool", rB, rA, float(1.0 / TWO_PI), None, ALU.mult, None, [b_rA], [b_rB])
        yield
        cp(c, "pool", kfi, rB, [b_rB], [b_kfi])
        yield
        cp(c, "pool", rB, kfi, [b_kfi], [b_rB])
        yield
        ts(c, "pool", rC, rB, -CW1, None, ALU.mult, None, [b_rB], [b_rC])
        yield
        tt(c, "pool", rC, rC, rA, ALU.add, [b_rC, b_rA], [b_rC])
        yield
        ts(c, "pool", rB, rB, -CW2, None, ALU.mult, None, [b_rB], [b_rB])
        yield
        tt(c, "pool", rC, rC, rB, ALU.add, [b_rC, b_rB], [b_rC])
        yield
        ts(c, "dve", rB, rC, float(np.pi), -TWO_PI, ALU.is_gt, ALU.mult, [b_rC], [b_rB])
        yield
        tt(c, "pool", rC, rC, rB, ALU.add, [b_rC, b_rB], [b_rC])
        yield
        ts(c, "dve", rB, rC, float(-np.pi), TWO_PI, ALU.is_lt, ALU.mult, [b_rC], [b_rB])
        yield
        tt(c, "pool", rC, rC, rB, ALU.add, [b_rC, b_rB], [b_rC])
        yield
        ts(c, "dve", rC, rC, PI_SAFE, -PI_SAFE, ALU.min, ALU.max, [b_rC], [b_rC])
        yield

    def rope_table(it):
        tab, b_tab = tabs[it % 2]
        act(c, tab, rC, AF.Sin, [b_rC], [b_tab])
        ts(c, "pool", tab[:, 0:TT], tab[:, 0:TT], c.consts[:, COL_SIGN:COL_SIGN + 1], None, ALU.mult, None, [b_tab, c.b_consts], [b_tab])

    for _ in rope_args(0):
        pass
    rope_table(0)
    nxt_tile = load_tile(c, c.x1_s, 0, [c.x1_bufs[0]])
    for it in range(T // TT):
        t0 = it * TT
        if it == 0:
            xt, bxt, xi = nxt_tile
            prenorm_T(c, xt, bxt)
        if it + 1 < T // TT:
            nxt_tile = load_tile(c, c.x1_s, it + 1, [c.x1_bufs[it + 1]])
        deferred = []
        late = []
        tm = []
        for s in range(2):
            lhs = lambda kc: c.xnT[:, kc, s * 128:(s + 1) * 128]
            psA, bA = c.pp.get()
            for kc in range(KC):
                mm(c, psA[:, 0:384], lhs(kc), c.wg[:, kc, 0:384], kc == 0, kc == KC - 1,
                   [c.b_xnT] + wbufs(c.b_wg, kc, 0, 384), [bA])
            psB, bB = c.pp.get()
            for kc in range(KC):
                mm(c, psB[:, 0:16], lhs(kc), c.wg[:, kc, W_AB0:W_AB0 + 16], kc == 0, kc == KC - 1,
                   [c.b_xnT] + wbufs(c.b_wg, kc, W_AB0, W_AB0 + 16), [bB])
            psC, bC = c.pp.get()
            for kc in range(KC):
                mm(c, psC[:, 0:512], lhs(kc), c.wg[:, kc, W_GATE0:W_GATE0 + 512], kc == 0, kc == KC - 1,
                   [c.b_xnT] + wbufs(c.b_wg, kc, W_GATE0, W_GATE0 + 512), [bC])
            tm.append((psA, bA, psB, bB, psC, bC))
        for s in range(2):
            psA, bA, psB, bB, psC, bC = tm[s]
            bst = c.b_st[10 + s]
            act(c, c.junk[:, 0:256], psA[:, 0:256], AF.Square, [bA], [c.b_junk, bst], accum_out=c.st[:, 10 + s:11 + s])
            rstd_from_ss(c, c.st[:, 10 + s:11 + s], c.st[:, 10 + s:11 + s], 256, 1.0, bst)
            ts(c, "dve", cqn[:, s, 0:256], psA[:, 0:256], c.st[:, 10 + s:11 + s], None, ALU.mult, None, [bA, bst], [b_cqn])
            bst2 = c.b_st[12 + s]
            act(c, c.junk[:, 256:384], psA[:, 256:384], AF.Square, [bA], [c.b_junk, bst2], accum_out=c.st[:, 12 + s:13 + s])
            rstd_from_ss(c, c.st[:, 12 + s:13 + s], c.st[:, 12 + s:13 + s], 128, 1.0, bst2)
            ts(c, "dve", cqn[:, s, 256:384], psA[:, 256:384], c.st[:, 12 + s:13 + s], None, ALU.mult, None, [bA, bst2], [b_cqn])
            def cq_T(s=s):
                psT, bT = c.pp.get()
                psTb = psT.bitcast(BF16)
                for i in range(3):
                    tr(c, psTb[:, i * 128:(i + 1) * 128], cqn[:, s, i * 128:(i + 1) * 128], c.ident_b[:], [b_cqn, c.b_ident], [bT])
                cp(c, "dve", cqnT[:, :, s * 128:(s + 1) * 128], psTb[:, 0:384].rearrange("p (k t) -> p k t", k=3), [bT], [b_cqnT])

            late.append(cq_T)
            zz = z8[:, s * 16:s * 16 + 8]
            tt(c, "dve", zz, psB[:, 0:8], c.consts[:, COL_DTB:COL_DTB + 8], ALU.add, [bB, c.b_consts], [b_z8])
            act(c, zz, zz, AF.Exp, [b_z8], [b_z8])
            act(c, zz, zz, AF.Ln, [b_z8, c.b_consts], [b_z8], bias=c.cb[:, 2:3])
            tt(c, "dve", gb[:, s, 0:8], zz, negA, ALU.mult, [b_z8, b_negA], [b_gb])
            deferred.append((s, psB, bB, psC, bC))
        tab, b_tab = tabs[it % 2]
        sin2, cos2 = tab[:, 0:TT], tab[:, TT:2 * TT]
        for s, psB, bB, psC, bC in deferred:
            act(c, gate_sb[:, s, :], psC[:, 0:512], AF.Silu, [bC], [b_gate])
            zt = z8[:, s * 16 + 8:s * 16 + 16]
            act(c, zt, psB[:, 8:16], AF.Tanh, [bB], [b_z8], scale=0.5)
            ts(c, "dve", gb[:, s, 8:16], zt, 0.5, 0.5, ALU.mult, ALU.add, [b_z8], [b_gb])
        bd = Buf("gb_s%d" % it)
        dma(c, c.gb_s[t0:t0 + TT, :].rearrange("(s p) n -> p s n", p=128), gb, [b_gb], [bd], "st_gb")
        gb_b.append(bd)
        def rope_rows(ps1, b1, ps2, b2, out_ap, out_b):
            (kt1, b_kt1), (kt2, b_kt2) = ktp.get()
            tt(c, "dve", kt1[64:96, :], ps1[64:96, 0:TT], cos2[64:96, :], ALU.mult, [b1, b_tab], [b_kt1])
            tt(c, "dve", kt2[64:96, :], ps2[64:96, 0:TT], sin2[64:96, :], ALU.mult, [b2, b_tab], [b_kt2])
            tt(c, "pool", out_ap[64:96, :], kt1[64:96, :], kt2[64:96, :], ALU.add, [b_kt1, b_kt2], [out_b])

        ps1, b1 = c.pp.get()
        ps2, b2 = c.pp.get()
        for kc in range(KC):
            mm(c, ps1[0:96, 0:TT], c.wg[:, kc, W_KPE0:W_KPE0 + 96], c.xnT[:, kc, :], kc == 0, kc == KC - 1,
               [c.b_xnT] + wbufs(c.b_wg, kc, W_KPE0, W_KPE0 + 96), [b1])
        for kc in range(KC):
            mm(c, ps2[0:96, 0:TT], c.wg[:, kc, W_KPESW0:W_KPESW0 + 96], c.xnT[:, kc, :], kc == 0, kc == KC - 1,
               [c.b_xnT] + wbufs(c.b_wg, kc, W_KPESW0, W_KPESW0 + 96), [b2])
        rope_rows(ps1, b1, ps2, b2, kpe_rot, b_kpe)
        bd = Buf("kpe_s%d" % it)
        dma(c, c.kpe_s[:, t0:t0 + TT], kpe_rot[64:96, :], [b_kpe], [bd], "st_kpe")
        kpe_b.append(bd)
        for fn_ in late:
            fn_()

        def conv_mm(ch):
            ps, bp = c.pp.get()
            c0 = W_QKV0 + ch * 128
            for kc in range(KC):
                mm(c, ps[:, 0:TT], c.wg[:, kc, c0:c0 + 128], c.xnT[:, kc, :], kc == 0, kc == KC - 1,
                   [c.b_xnT] + wbufs(c.b_wg, kc, c0, c0 + 128), [bp])
            return ps, bp

        def conv_x(ch, ps, bp):
            xc, bxc = xcp.get()
            hl, bhl = hal[ch]
            cp(c, "act", xc[:, 3:3 + TT], ps[:, 0:TT], [bp], [bxc])
            cp(c, "act", xc[:, 0:3], hl[:, 0:3], [bhl], [bxc])
            cp(c, "act", hl[:, 0:3], xc[:, TT:TT + 3], [bxc], [bhl])
            acc, bacc = accp.get()
            act(c, acc, ps[:, 0:TT], AF.Copy, [bp, c.b_consts], [bacc],
                scale=c.consts[:, COL_CONV + ch * 4 + 3:COL_CONV + ch * 4 + 4])
            return xc, bxc, acc, bacc

        def conv_post(ch, xc, bxc, acc, bacc):
            wcol = lambda j: c.consts[:, COL_CONV + ch * 4 + j:COL_CONV + ch * 4 + j + 1]
            for j in (2, 1, 0):
                stt(c, acc, xc[:, j:j + TT], wcol(j), acc, ALU.mult, ALU.add, [bxc, bacc, c.b_consts], [bacc])
            if ch < 8:
                y, by = ys[ch]
                act(c, y, acc, AF.Silu, [bacc], [by])
            else:
                feat, bfeat = featp.get()
                act(c, feat, acc, AF.Silu, [bacc], [bfeat])
                to_tok(ch, feat, bfeat)

        def to_tok(ch, feat, bfeat):
            pst, bt = c.pp.get()
            pstb = pst.bitcast(BF16)
            for s in range(2):
                tr(c, pstb[:, s * 128:(s + 1) * 128], feat[:, s * 128:(s + 1) * 128], c.ident_b[:], [bfeat, c.b_ident], [bt])
            cc = (ch - 4) * 128
            cp(c, "dve", tok[:, :, cc:cc + 128], pstb[:, 0:256].rearrange("p (s n) -> p s n", s=2), [bt], [b_tok])

        def mla_head(h):
            ps1, b1 = c.pp.get()
            ps2, b2 = c.pp.get()
            for kc in range(2):
                mm(c, ps1[0:96, 0:TT], wuq[:, kc, h * 192:h * 192 + 96], cqnT[:, kc, :], kc == 0, kc == 1, [b_wuq, b_cqnT], [b1])
            for kc in range(2):
                mm(c, ps2[0:96, 0:TT], wuq[:, kc, h * 192 + 96:h * 192 + 192], cqnT[:, kc, :], kc == 0, kc == 1, [b_wuq, b_cqnT], [b2])
            ps3, b3 = c.pp.get()
            mm(c, ps3[0:64, 0:TT], wukv[:, h * 64:(h + 1) * 64], cqnT[:, 2, :], True, True, [b_wukv, b_cqnT], [b3])
            q_sb, bq = qp.get()
            cp(c, "act", q_sb[0:64, :], ps1[0:64, 0:TT], [b1], [bq])
            rope_rows(ps1, b1, ps2, b2, q_sb, bq)
            bd = Buf("qT_s%d_%d" % (h, it))
            dma(c, c.qT_s[h][:, t0:t0 + TT], q_sb[0:96, :], [bq], [bd], "st_q%d" % ((qp.i - 1) % 3))
            qT_b.append(bd)
            kn, bkn = knp.get()
            cp(c, "act", kn[0:64, :], ps3[0:64, 0:TT], [b3], [bkn])
            bd = Buf("kn_s%d_%d" % (h, it))
            dma(c, c.knope_s[h][:, t0:t0 + TT], kn[0:64, :], [bkn], [bd], "st_kn%d" % ((knp.i - 1) % 3))
            kn_b.append(bd)

        rg = rope_args(it + 1) if it + 1 < T // TT else iter(())
        pm = {0: conv_mm(0), 1: conv_mm(1)}
        px = {0: conv_x(0, *pm.pop(0))}
        for ch in range(12):
            if ch + 2 < 12:
                pm[ch + 2] = conv_mm(ch + 2)
            if ch + 1 < 12:
                px[ch + 1] = conv_x(ch + 1, *pm.pop(ch + 1))
            conv_post(ch, *px.pop(ch))
            next(rg, None)
            next(rg, None)
            if ch < 8:
                mla_head(ch)
        for _ in rg:
            pass
        if it + 1 < T // TT:
            rope_table(it + 1)
            xt, bxt, xi = nxt_tile
            prenorm_T(c, xt, bxt)
        for s in range(2):
            ps4, b4 = c.pp.get()
            mm(c, ps4[:, 0:512], cqnT[:, 2, s * 128:(s + 1) * 128], wukv[:, 512:1024], True, True, [b_wukv, b_cqnT], [b4])
            cp(c, "dve", v_sb[:, s, :], ps4[:, 0:512], [b4], [b_vsb])
        bd = Buf("v_s%d" % it)
        dma(c, c.v_s[:, 2 * it:2 * it + 2, :], v_sb, [b_vsb], [bd], "st_v")
        v_b.append(bd)
        def norm_a(ch):
            y, by = ys[ch]
            sq, bsq = sqp.get()
            tt(c, "pool", sq, y, y, ALU.mult, [by], [bsq])
            psn, bn = c.pp.get()
            mm(c, psn[:, 0:TT], blk, sq, True, True, [b_blk, bsq], [bn])
            return sq, bsq, psn, bn

        pn = {0: norm_a(0), 1: norm_a(1)}
        for ch in range(8):
            if ch + 2 < 8:
                pn[ch + 2] = norm_a(ch + 2)
            y, by = ys[ch]
            sq, bsq, psn, bn = pn.pop(ch)
            act(c, sq, psn[:, 0:TT], AF.Ln, [bn, c.b_consts], [bsq], bias=c.cb[:, 0:1])
            act(c, sq, sq, AF.Exp, [bsq, c.b_consts], [bsq], scale=-0.5, bias=(c.cb[:, 1:2] if ch < 4 else c.cb[:, 3:4]))
            feat, bfeat = featp.get()
            tt(c, "dve", feat, y, sq, ALU.mult, [by, bsq], [bfeat])
            dst, lst = (c.gq_s, gq_b) if ch < 4 else (c.gk_s, gk_b)
            bd = Buf("g%d_%d" % (ch, it))
            dma(c, dst[ch % 4][:, t0:t0 + TT], feat, [bfeat], [bd], "st_feat%d_%d" % ((featp.i - 1) % 4, ch // 4))
            lst.append(bd)
            if ch >= 4:
                to_tok(ch, feat, bfeat)
        for s in range(2):
            psg, bg_ = c.pp.get()
            mm(c, psg[:, 0:8], mui, gb[:, s, 0:8], True, True, [b_mui, b_gb], [bg_])
            cp(c, "dve", cmb[:, s, 0:8], psg[:, 0:8], [bg_], [b_cmb])
            cp(c, "pool", cmb[:, s, 8:16], gb[:, s, 8:16], [b_gb], [b_cmb])
            pst_, bt_ = c.pp.get()
            tr(c, pst_[0:16, 0:128], cmb[:, s, :], c.ident_f[:], [b_cmb, c.b_ident], [bt_])
            cp(c, "dve", gbT[0:16, s, :], pst_[0:16, 0:128], [bt_], [b_gbT])
        bd = Buf("gbT_s%d" % it)
        dma(c, c.gbT_s[2 * it:2 * it + 2].rearrange("s h i -> h s i"), gbT[0:16, :, :], [b_gbT], [bd], "st_gbT")
        gbT_b.append(bd)
        bd = Buf("gate_s%d" % it)
        dma(c, c.gate_s[t0:t0 + TT, :].rearrange("(s p) n -> p s n", p=128), gate_sb, [b_gate], [bd], "st_gate")
        gate_b.append(bd)

        bd = Buf("kvtok_s%d" % it)
        dma(c, c.kvtok_s[t0:t0 + TT, :].rearrange("(s p) n -> p s n", p=128), tok, [b_tok], [bd], "st_tok")
        kvtok_b.append(bd)
    c.p2 = dict(gq=gq_b, gk=gk_b, kvtok=kvtok_b, gb=gb_b, gate=gate_b, qT=qT_b, kn=kn_b, kpe=kpe_b, v=v_b, gbT=gbT_b)
    for l in c.p2.values():
        c.dbg_final += l
    barrier(c, allb, R.bufs + R2.bufs)
    return


def phase3(c, pump=None):
    S, T, NT = c.S, c.T, c.NT
    R = c.reg_d
    R.reset()
    allb = []

    def alloc(n, dt):
        ap, _ = R.alloc(n, dt)
        b = Buf("p3_%d" % len(allb))
        allb.append(b)
        return ap, b

    c.pumped = 0
    kT = [alloc(T, BF16) for _ in range(2)]
    V = [alloc(NT * 65, BF16) for _ in range(2)]
    qb = RPool([alloc(512, BF16) for _ in range(2)])
    PT = RPool([alloc(512, BF16) for _ in range(4)])
    oT = RPool([alloc(512, F32) for _ in range(2)])
    osb = RPool([alloc(256, F32) for _ in range(2)])
    rden, b_rden = alloc(4, F32)
    tri_f, b_trif = alloc(128, F32)
    tri, b_tri = alloc(128, BF16)
    barrier(c, R.bufs, allb)
    dma(c, tri_f, c.masks_d[:, 5, :], [], [b_trif], "trif")
    cp(c, "dve", tri, tri_f, [b_trif], [b_tri])
    pp_s = RPool([(c.ps[i], c.b_ps[i]) for i in range(4)])
    pp_o = RPool([(c.ps[i], c.b_ps[i]) for i in (4, 5)])
    pp_t = RPool([(c.ps[i], c.b_ps[i]) for i in (6, 7)])
    scale = float(96 ** -0.5)
    NQB = T // 512
    out_bufs = []
    p2 = c.p2
    def load_kv(h):
        kt, bkt = kT[h % 2]
        v, bv = V[h % 2]
        v3 = v.rearrange("p (j e) -> p j e", e=65)
        dma(c, kt[0:64, :], c.knope_s[h], p2["kn"], [bkt], "kt%d" % (h % 2))
        dma(c, kt[64:96, :], c.kpe_s, p2["kpe"], [bkt], "kt%d" % (h % 2))
        dma(c, v3[:, :, 0:64], c.v_s[:, :, h * 64:(h + 1) * 64], p2["v"], [bv], "v%d" % (h % 2))
        S.emit("pool", (lambda e, v3=v3: e.memset(v3[:, :, 64:65], 1.0)), writes=[bv])

    def load_q(h, I):
        q, bq = qb.get()
        dma(c, q[0:96, :], c.qT_s[h][:, I * 512:(I + 1) * 512], p2["qT"], [bq], "qb%d" % ((qb.i - 1) % 2))
        return q, bq

    items = [(h, I) for h in range(NH) for I in range(NQB)]
    load_kv(0)
    qn = load_q(*items[0])
    for ii, (h, I) in enumerate(items):
        kt, bkt = kT[h % 2]
        v, bv = V[h % 2]
        v3 = v.rearrange("p (j e) -> p j e", e=65)
        if I == 0 and h + 1 < NH:
            load_kv(h + 1)
        if True:
            if pump is not None and c.pumped < KC:
                next(pump, None)
                c.pumped += 1
            elif c.wo_next < KC:
                load_wo_row(c)
            q, bq = qn
            if ii + 1 < len(items):
                qn = load_q(*items[ii + 1])
            nj = 4 * I + 4
            psO, bO = pp_o.get()
            sres = {}

            def emit_s(j):
                ps, bp = pp_s.get()
                c0 = max(0, j - 4 * I) * 128
                w = 512 - c0
                mm(c, ps[:, 0:w], kt[0:96, j * 128:(j + 1) * 128], q[0:96, c0:512], True, True, [bkt, bq], [bp])
                sres[j] = (ps, bp, c0, w)

            def emit_pv(j):
                ps, bp, c0, w = sres.pop(j)
                pt, bpt = PT.get()
                act(c, pt[:, 0:w], ps[:, 0:w], AF.Exp, [bp], [bpt], scale=scale)
                if j >= 4 * I:
                    tt(c, "pool", pt[:, 0:128], pt[:, 0:128], tri, ALU.mult, [bpt, b_tri], [bpt])
                mm(c, psO[0:65, c0:512], v3[:, j, 0:65], pt[:, 0:w], j == 0, j == nj - 1, [bv, bpt], [bO])

            for j0 in range(min(3, nj)):
                emit_s(j0)
            for j in range(nj):
                if j + 3 < nj:
                    emit_s(j + 3)
                emit_pv(j)
            o_f, bof = oT.get()
            cp(c, "dve", o_f[0:65, :], psO[0:65, :], [bO], [bof])
            psT, bT = pp_t.get()
            psT4 = psT[:].rearrange("p (q e) -> p q e", q=4)
            for qq in range(4):
                tr(c, psT4[:, qq, 0:65], o_f[0:65, qq * 128:(qq + 1) * 128], c.ident_f[0:65, 0:65], [bof, c.b_ident], [bT])
            S.emit("dve", (lambda e, psT4=psT4: e.reciprocal(out=rden.unsqueeze(2), in_=psT4[:, :, 64:65])),
                   reads=[bT], writes=[b_rden])
            o_sb, bo = osb.get()
            o3 = o_sb.rearrange("p (q e) -> p q e", q=4)
            tt(c, "dve", o3, psT4[:, :, 0:64], rden.unsqueeze(2).to_broadcast([128, 4, 64]), ALU.mult, [bT, b_rden], [bo])
            bd = Buf("mla_s%d_%d" % (h, I))
            dma(c, c.mla_s[I * 512:(I + 1) * 512, h * 64:(h + 1) * 64].rearrange("(q p) e -> p q e", p=128), o3,
                [bo], [bd], "st_o%d" % ((osb.i - 1) % 2))
            out_bufs.append(bd)
    c.mla_bufs = out_bufs
    c.dbg_final += out_bufs
    barrier(c, allb, R.bufs)


def phase4(c):
    S_, T, NT = c.S, c.T, c.NT
    RU, RD = c.reg_u, c.reg_d
    RU.reset()
    RD.reset()
    allb = []
    state = {"r": RU}

    def alloc(n, dt):
        esz = 4 if dt in (F32, I32) else 2
        r = state["r"]
        if (r.off + 31) // 32 * 32 + n * esz > r.nbytes:
            state["r"] = r = RD
        ap, _ = r.alloc(n, dt)
        b = Buf("p4_%d" % len(allb))
        allb.append(b)
        return ap, b

    def a3(n_outer, n_inner, dt):
        ap, b = alloc(n_outer * n_inner, dt)
        return ap.rearrange("p (h i) -> p h i", h=n_outer), b

    def nb(name):
        b = Buf(name)
        allb.append(b)
        return b

    QKp = RPool([alloc(8 * 2 * 128, BF16) for _ in range(2)])
    KVp = RPool([alloc(1024, BF16) for _ in range(2)])
    gbp = RPool([alloc(16, F32) for _ in range(2)])
    gatep = RPool([alloc(512, F32) for _ in range(2)])
    Gb, b_Gb = a3(16, 128, F32)
    MUi, b_MUi = alloc(128, F32)
    NMUi, b_NMUi = alloc(128, F32)
    MUs, b_MUs = alloc(128, F32)
    NMLs, b_NMLs = alloc(128, F32)
    BLK, b_BLK = alloc(128, F32)
    ones, b_ones = alloc(128, F32)
    gcgl, b_gcgl = alloc(16, F32)
    sm, b_sm = alloc(32, F32)
    EU, _ = a3(8, 128, F32)
    EL, _ = a3(8, 128, F32)
    ExpG, b_ExpG = a3(8, 128, F32)
    tmpA, b_tmpA = a3(8, 128, F32)
    tmpB, b_tmpB = a3(8, 128, F32)
    Dg, b_Dg = ExpG, b_ExpG
    Db, b_Db = tmpB, b_tmpB
    BM, b_BM = tmpA, b_tmpA
    Nm, _ = a3(8, 128, F32)
    Lm, _ = a3(8, 128, F32)
    Pm, Rm = EU, EL
    bA2 = [nb("p4tA%d" % i) for i in range(2)]
    bB2 = [nb("p4tB%d" % i) for i in range(2)]
    bX2 = [nb("p4X%d" % i) for i in range(2)]
    bN = [nb("p4N%d" % i) for i in range(2)]
    bL = [nb("p4L%d" % i) for i in range(2)]
    bP = [nb("p4P%d" % i) for i in range(2)]
    bR = [nb("p4R%d" % i) for i in range(2)]
    Pb, b_Pb = a3(8, 128, BF16)
    Kbg, b_Kbg = alloc(512, BF16)
    Vb, b_Vb = alloc(512, BF16)
    HO = []
    for i in range(2):
        h = Ctx()
        h.attnT, h.b_attnT = a3(8, 128, BF16)
        h.wT, h.b_wT = a3(8, 128, BF16)
        h.QdT, h.b_QdT = a3(8, 128, BF16)
        h.Kd, h.b_Kd = alloc(512, BF16)
        h.u, h.b_u = alloc(512, F32)
        h.glS, h.b_glS = alloc(16, F32)
        HO.append(h)
    vnew, b_vnew = alloc(512, BF16)
    o_sb, b_o = alloc(512, F32)
    rs, b_rs = alloc(8, F32)
    Sf, b_Sf = alloc(512, F32)
    Sb, b_Sb = alloc(512, BF16)
    Stmp, b_Stmp = alloc(512, F32)
    barrier(c, RU.bufs + RD.bufs, allb)
    for i, (ap, b) in enumerate(((MUi, b_MUi), (NMUi, b_NMUi), (MUs, b_MUs), (NMLs, b_NMLs), (BLK, b_BLK))):
        dma(c, ap, c.masks_d[:, i, :], [], [b], "p4m%d" % i)
    S_.emit("pool", (lambda e: e.memset(ones, 1.0)), writes=[b_ones])
    S_.emit("pool", (lambda e: e.memset(Sf, 0.0)), writes=[b_Sf])
    S_.emit("pool", (lambda e: e.memset(Sb, 0.0)), writes=[b_Sb])
    identb = c.ident_f[:].unsqueeze(1)
    p2 = c.p2
    out_bufs = []
    c.gdn_bufs = out_bufs
    H4 = lambda ap, hh: ap[:, 4 * hh:4 * hh + 4, :]
    bc4 = lambda col_ap: col_ap.unsqueeze(2).to_broadcast([128, 4, 128])
    m4 = lambda m: m.unsqueeze(1).to_broadcast([128, 4, 128])
    bc8 = lambda col: col.unsqueeze(2).to_broadcast([128, 8, 64])
    v4 = lambda ps: ps[:].rearrange("p (h i) -> p h i", h=4)
    ppP = RPool([(c.ps[i], c.b_ps[i]) for i in range(5)])
    ppS = RPool([(c.ps[i], c.b_ps[i]) for i in (5, 6, 7)])

    def loads(t):
        t0 = t * 128
        L = Ctx()
        L.QK, L.b_QK = QKp.get()
        L.QK5 = L.QK.rearrange("p (c a k t) -> p c a k t", c=4, a=2, k=2)
        key = "p4qk%d" % ((QKp.i - 1) % 2)
        for hp in range(2):
            dma(c, L.QK5[0:64, :, hp, 0, :], c.gk_s[:, hp * 64:(hp + 1) * 64, t0:t0 + 128].rearrange("c d t -> d c t"),
                p2["gk"], [L.b_QK], key)
            dma(c, L.QK5[0:64, :, hp, 1, :], c.gq_s[:, hp * 64:(hp + 1) * 64, t0:t0 + 128].rearrange("c d t -> d c t"),
                p2["gq"], [L.b_QK], key)
        L.KV, L.b_KV = KVp.get()
        dma(c, L.KV, c.kvtok_s[t0:t0 + 128, :], p2["kvtok"], [L.b_KV], "p4kv%d" % ((KVp.i - 1) % 2))
        L.gbt, L.b_gbt = gbp.get()
        dma(c, L.gbt, c.gb_s[t0:t0 + 128, :], p2["gb"], [L.b_gbt], "p4gb%d" % ((gbp.i - 1) % 2))
        dma(c, Gb, c.gbT_s[t].partition_broadcast(128), p2["gbT"], [b_Gb], "p4Gb")
        L.gatet, L.b_gatet = gatep.get()
        dma(c, L.gatet, c.gate_s[t0:t0 + 128, :], p2["gate"], [L.b_gatet], "p4gate%d" % ((gatep.i - 1) % 2))
        return L

    def prep_gen(t, L, ho):
        QK5, b_QK, KV, b_KV, gbt, b_gbt = L.QK5, L.b_QK, L.KV, L.b_KV, L.gbt, L.b_gbt
        beta = gbt[:, 8:16]
        ps, bp = ppP.get()
        mm(c, ps[:, 0:8], MUi, gbt[:, 0:8], True, True, [b_MUi, b_gbt], [bp])
        mm(c, ps[:, 8:16], BLK, gbt[:, 0:8], True, True, [b_BLK, b_gbt], [bp])
        cp(c, "act", gcgl, ps[:, 0:16], [bp], [b_gcgl])
        gc = gcgl[:, 0:8]
        act(c, sm[:, 0:8], gc, AF.Exp, [b_gcgl], [b_sm])
        tt(c, "dve", sm[:, 24:32], gcgl[:, 8:16], gc, ALU.subtract, [b_gcgl], [b_sm])
        act(c, sm[:, 8:16], sm[:, 24:32], AF.Exp, [b_sm], [b_sm])
        tt(c, "dve", sm[:, 16:24], sm[:, 0:8], beta, ALU.mult, [b_sm, b_gbt], [b_sm])
        K3 = KV[:, 0:512].rearrange("p (h e) -> p h e", h=8)
        V3 = KV[:, 512:1024].rearrange("p (h e) -> p h e", h=8)
        tt(c, "pool", Kbg.rearrange("p (h e) -> p h e", h=8), K3, bc8(sm[:, 16:24]), ALU.mult, [b_KV, b_sm], [b_Kbg])
        tt(c, "pool", ho.Kd.rearrange("p (h e) -> p h e", h=8), K3, bc8(sm[:, 8:16]), ALU.mult, [b_KV, b_sm], [ho.b_Kd])
        tt(c, "pool", Vb.rearrange("p (h e) -> p h e", h=8), V3, bc8(beta), ALU.mult, [b_KV, b_gbt], [b_Vb])
        yield
        G = [(Gb[:, 4 * hh:4 * hh + 4, :], b_Gb) for hh in range(2)]
        Bb = [(Gb[:, 8 + 4 * hh:12 + 4 * hh, :], b_Gb) for hh in range(2)]
        def e_ops(hh):
            g3, bg = G[hh]
            b3, bb = Bb[hh]
            return [
                lambda: tt(c, "dve", H4(tmpA, hh), g3, bc4(gc[:, 4 * hh:4 * hh + 4]), ALU.subtract, [bg, b_gcgl], [bA2[hh]]),
                lambda: act(c, H4(ExpG, hh), g3, AF.Exp, [bg], [bX2[hh]]),
                lambda: tt(c, "dve", H4(tmpB, hh), H4(tmpA, hh), m4(NMUi), ALU.min, [bA2[hh], b_NMUi], [bB2[hh]]),
                lambda: act(c, H4(EU, hh), H4(tmpB, hh), AF.Exp, [bB2[hh]], [bP[hh]]),
                lambda: stt(c, H4(tmpB, hh), H4(tmpA, hh), -1.0, m4(NMLs), ALU.mult, ALU.min, [bA2[hh], b_NMLs], [bB2[hh]]),
                lambda: act(c, H4(EL, hh), H4(tmpB, hh), AF.Exp, [bB2[hh]], [bR[hh]]),
                lambda: tt(c, "pool", H4(BM, hh), b3, m4(MUs), ALU.mult, [bb, b_MUs], [bA2[hh]]),
            ]

        for fa, fb in zip(e_ops(0), e_ops(1)):
            fa()
            fb()
        yield
        tt(c, "pool", ho.QdT[0:64].rearrange("p (c two) i -> p c two i", two=2), QK5[0:64, :, :, 1, :],
           ExpG[0:64].rearrange("p (c two) i -> p c two i", two=2), ALU.mult, [b_QK] + bX2, [ho.b_QdT])
        for ch in range(2):
            cp(c, "pool", ho.glS[0:64, ch * 8:(ch + 1) * 8], ExpG[0:64, :, ch * 64 + 63], bX2, [ho.b_glS])
        KQ = []
        for hh in range(2):
            psK, bK = ppP.get()
            psQ, bQ = ppP.get()
            for hl in range(4):
                h = 4 * hh + hl
                pr, hp = h // 2, h % 2
                kt = QK5[0:64, pr, hp, 0, :]
                qt = QK5[0:64, pr, hp, 1, :]
                mm(c, psK[:, hl * 128:(hl + 1) * 128], kt, kt, True, True, [b_QK], [bK])
                mm(c, psQ[:, hl * 128:(hl + 1) * 128], kt, qt, True, True, [b_QK], [bQ])
            KQ.append((v4(psK), bK, v4(psQ), bQ))

        def k_ops(hh):
            k3, bK, q3, bQ = KQ[hh]
            return [
                lambda: tt(c, "dve", H4(Nm, hh), k3, H4(EU, hh), ALU.mult, [bK, bP[hh]], [bN[hh]]),
                lambda: tt(c, "dve", H4(Lm, hh), k3, H4(EL, hh), ALU.mult, [bK, bR[hh]], [bL[hh]]),
                lambda: tt(c, "pool", H4(Nm, hh), H4(Nm, hh), H4(BM, hh), ALU.mult, [bN[hh], bA2[hh]], [bN[hh]]),
                lambda: tt(c, "dve", H4(ho.attnT, hh), q3, H4(EU, hh), ALU.mult, [bQ, bP[hh]], [ho.b_attnT]),
                lambda: tt(c, "pool", H4(Lm, hh), H4(Lm, hh), bc4(beta[:, 4 * hh:4 * hh + 4]), ALU.mult, [bL[hh], b_gbt], [bL[hh]]),
                lambda: tt(c, "pool", H4(Pm, hh), identb.to_broadcast([128, 4, 128]), H4(Nm, hh), ALU.subtract,
                           [c.b_ident, bN[hh]], [bP[hh]]),
                lambda: tt(c, "pool", H4(Rm, hh), identb.to_broadcast([128, 4, 128]), H4(Lm, hh), ALU.subtract,
                           [c.b_ident, bL[hh]], [bR[hh]]),
            ]

        for fa, fb in zip(k_ops(0), k_ops(1)):
            fa()
            fb()
        yield
        for k in range(1, 6):
            last = (k == 5)
            sq = []
            for hh in range(2):
                psN, bpN = ppP.get()
                for hl in range(4):
                    h = 4 * hh + hl
                    mm(c, psN[:, hl * 128:(hl + 1) * 128], Lm[:, h, :], Nm[:, h, :], True, True, [bL[hh], bN[hh]], [bpN])
                sq.append((psN, bpN))
            for hh in range(2):
                psN, bpN = sq[hh]
                cp(c, "act", H4(Nm, hh), v4(psN), [bpN], [bN[hh]])
            yield
            tl, up = [], []
            for hh in range(2):
                if not last:
                    psL, bpL = ppP.get()
                    for hl in range(4):
                        h = 4 * hh + hl
                        tr(c, psL[:, hl * 128:(hl + 1) * 128], Nm[:, h, :], c.ident_f[:], [bN[hh], c.b_ident], [bpL])
                    tl.append((psL, bpL))
                psP, bpP = ppP.get()
                for hl in range(4):
                    h = 4 * hh + hl
                    mm(c, psP[:, hl * 128:(hl + 1) * 128], Rm[:, h, :], Nm[:, h, :], True, True, [bR[hh], bN[hh]], [bpP])
                up.append((psP, bpP))
            for hh in range(2):
                if not last:
                    psL, bpL = tl[hh]
                    cp(c, "act", H4(Lm, hh), v4(psL), [bpL], [bL[hh]])
                psP, bpP = up[hh]
                tt(c, "dve", H4(Pm, hh), H4(Pm, hh), v4(psP), ALU.add, [bP[hh], bpP], [bP[hh]])
            yield
            if not last:
                tr_ = []
                for hh in range(2):
                    psR, bpR = ppP.get()
                    for hl in range(4):
                        h = 4 * hh + hl
                        tr(c, psR[:, hl * 128:(hl + 1) * 128], Pm[:, h, :], c.ident_f[:], [bP[hh], c.b_ident], [bpR])
                    tr_.append((psR, bpR))
                for hh in range(2):
                    psR, bpR = tr_[hh]
                    cp(c, "act", H4(Rm, hh), v4(psR), [bpR], [bR[hh]])
                yield
        for hh in range(2):
            cp(c, "act", H4(Pb, hh), H4(Pm, hh), [bP[hh]], [b_Pb])
        psU, bU = ppP.get()
        for hh in range(2):
            psW, bW = ppP.get()
            for hl in range(4):
                h = 4 * hh + hl
                mm(c, psW[0:64, hl * 128:(hl + 1) * 128], Kbg[:, h * 64:(h + 1) * 64], Pb[:, h, :], True, True,
                   [b_Kbg, b_Pb], [bW])
            cp(c, "act", ho.wT[0:64, 4 * hh:4 * hh + 4, :], psW[0:64, :].rearrange("p (c i) -> p c i", c=4), [bW], [ho.b_wT])
        for h in range(8):
            mm(c, psU[:, h * 64:(h + 1) * 64], Pb[:, h, :], Vb[:, h * 64:(h + 1) * 64], True, True, [b_Pb, b_Vb], [bU])
        cp(c, "act", ho.u, psU[:, 0:512], [bU], [ho.b_u])
        yield

    def scan_gen(t, L, ho):
        t0 = t * 128
        attnT, wT, QdT, Kd, u, glS = ho.attnT, ho.wT, ho.QdT, ho.Kd, ho.u, ho.glS
        for ch in range(2):
            rows = slice(ch * 64, (ch + 1) * 64)
            cols = slice(ch * 64, (ch + 1) * 64)
            psA, bA = ppS.get()
            psB, bB = ppS.get()
            for h in range(8):
                sb_h = Sb[0:64, h * 64:(h + 1) * 64]
                mm(c, psA[rows, h * 64:(h + 1) * 64], wT[0:64, h, cols], sb_h, True, True, [ho.b_wT, b_Sb], [bA])
            for h in range(8):
                sb_h = Sb[0:64, h * 64:(h + 1) * 64]
                mm(c, psB[rows, h * 64:(h + 1) * 64], QdT[0:64, h, cols], sb_h, True, True, [ho.b_QdT, b_Sb], [bB])
            tt(c, "dve", vnew[rows, :], u[rows, :], psA[rows, 0:512], ALU.subtract, [ho.b_u, bA], [b_vnew])
            cp(c, "act", o_sb[rows, :], psB[rows, 0:512], [bB], [b_o])
            tt(c, "pool", Stmp[0:64].rearrange("p (c e) -> p c e", c=8), Sf[0:64].rearrange("p (c e) -> p c e", c=8),
               glS[0:64, ch * 8:(ch + 1) * 8].unsqueeze(2).to_broadcast([64, 8, 64]), ALU.mult, [b_Sf, ho.b_glS], [b_Stmp])
            yield
            psD, bD = ppS.get()
            for h in range(8):
                vn_h = vnew[rows, h * 64:(h + 1) * 64]
                mm(c, psD[0:64, h * 64:(h + 1) * 64], Kd[rows, h * 64:(h + 1) * 64], vn_h, True, True, [ho.b_Kd, b_vnew], [bD])
            tt(c, "dve", Sf[0:64], Stmp[0:64], psD[0:64, 0:512], ALU.add, [b_Stmp, bD], [b_Sf])
            cp(c, "act", Sb[0:64], Sf[0:64], [b_Sf], [b_Sb])
            psC, bC = ppS.get()
            for h in range(8):
                vn_h = vnew[rows, h * 64:(h + 1) * 64]
                mm(c, psC[rows, h * 64:(h + 1) * 64], attnT[rows, h, cols], vn_h, True, True, [ho.b_attnT, b_vnew], [bC])
            tt(c, "dve", o_sb[rows, :], o_sb[rows, :], psC[rows, 0:512], ALU.add, [b_o, bC], [b_o])
            yield
        osq, b_osq = Stmp, b_Stmp
        tt(c, "pool", osq, o_sb, o_sb, ALU.mult, [b_o], [b_osq])
        S_.emit("dve", (lambda e: e.tensor_reduce(out=rs, in_=osq.rearrange("p (h e) -> p h e", h=8), axis=AX.X, op=ALU.add)),
                reads=[b_osq], writes=[b_rs])
        rstd_from_ss(c, rs, rs, 64, 1.0, b_rs)
        yield
        o3 = o_sb.rearrange("p (h e) -> p h e", h=8)
        tt(c, "dve", o3, o3, bc8(rs), ALU.mult, [b_o, b_rs], [b_o])
        tt(c, "pool", o_sb, o_sb, L.gatet, ALU.mult, [b_o, L.b_gatet], [b_o])
        bd = Buf("gdn_s%d" % t)
        dma(c, c.gdn_s[t0:t0 + 128, :], o_sb, [b_o], [bd], "st_gdn")
        out_bufs.append(bd)
        yield

    Ls = {0: loads(0)}
    stop = getattr(c, "p4_stop", 99)
    for i, _ in enumerate(prep_gen(0, Ls[0], HO[0])):
        if i + 1 >= stop:
            break
    for t in range(NT if stop >= 50 else 0):
        gp = iter(())
        if t + 1 < NT:
            Ls[t + 1] = loads(t + 1)
            gp = prep_gen(t + 1, Ls[t + 1], HO[(t + 1) % 2])
        gs = scan_gen(t, Ls[t], HO[t % 2])
        if getattr(c, "p4_serial", False):
            for _ in gs:
                pass
            for _ in gp:
                pass
        pa = sa = not getattr(c, "p4_serial", False)
        while pa or sa:
            if pa:
                pa = next(gp, "end") != "end"
            if pa:
                pa = next(gp, "end") != "end"
            if sa:
                sa = next(gs, "end") != "end"
        Ls.pop(t)
        if getattr(c, "op_on", False) and t % 2 == 1:
            it = t // 2
            if it >= 1:
                outproj_compute(c, it - 1)
            outproj_loads(c, it)
    if getattr(c, "op_on", False):
        outproj_compute(c, NT // 2 - 1)
    c.gdn_bufs = out_bufs
    c.dbg_final += out_bufs
    barrier(c, allb, RU.bufs + RD.bufs)


def phase5(c, pump_unused):
    S, T = c.S, c.T
    for _ in pump_unused:
        pass
    while c.wo_next < KC:
        load_wo_row(c)
    R = c.reg_u
    R.reset()
    allb = []

    def alloc(n, dt):
        ap, _ = R.alloc(n, dt)
        b = Buf("p5_%d" % len(allb))
        allb.append(b)
        return ap, b

    mop = RPool([alloc(1024, F32) for _ in range(2)])
    gop = RPool([alloc(1024, F32) for _ in range(2)])
    barrier(c, R.bufs, allb)
    load_gbc(c, 1)
    wd_src = c.w["ffn2_d"]
    wd_pieces = [(f, c0, min(D, c0 + PIECE)) for f in range(NF) for c0 in range(0, D, PIECE)]
    wd_state = {"i": 0, "pend": []}

    def wd_issue(k):
        for _ in range(k):
            if wd_state["i"] >= len(wd_pieces):
                return
            f, c0, c1 = wd_pieces[wd_state["i"]]
            wd_state["i"] += 1
            si = c.stg_i % NSTG
            c.stg_i += 1
            stg, bst = c.stg[si], c.b_stg[si]
            dma(c, stg[:, 0:c1 - c0], wd_src[f * 128:(f + 1) * 128, c0:c1], [], [bst], "stg%d" % si)
            wd_state["pend"].append((f, c0, c1, stg, bst))

    def wd_cast():
        for f, c0, c1, stg, bst in wd_state["pend"]:
            cast_scaled(c, c.wd[:, f, c0:c1], stg[:, 0:c1 - c0], None, [bst], [c.b_wd[f]])
        wd_state["pend"] = []
    n = T // TT

    def loads(it):
        t0 = it * TT
        x = load_tile(c, c.x1_s, it, [c.x1_bufs[it]])
        mo, bmo = mop.get()
        go, bgo = gop.get()
        k = (mop.i - 1) % 2
        mo3 = mo.rearrange("p (s n) -> p s n", s=2)
        go3 = go.rearrange("p (s n) -> p s n", s=2)
        dma(c, mo3, c.mla_s[t0:t0 + TT, :].rearrange("(s p) n -> p s n", p=128), c.mla_bufs, [bmo], "p5mo%d" % k)
        dma(c, go3, c.gdn_s[t0:t0 + TT, :].rearrange("(s p) n -> p s n", p=128), c.gdn_bufs, [bgo], "p5go%d" % k)
        return x, (mo3, bmo), (go3, bgo)

    x2_bufs = []

    def pre(ld):
        (xt, bxt, xi), (mo3, bmo), (go3, bgo) = ld
        for s in range(2):
            bst = c.b_st[14 + s]
            ssap = c.st[:, 14 + s:15 + s]
            act(c, c.junk[:, 0:512], mo3[:, s, :], AF.Square, [bmo], [c.b_junk, bst], accum_out=ssap)
            rstd_from_ss(c, ssap, ssap, 512, 1.0, bst)
            ts(c, "dve", c.xn[:, s, 0:512], mo3[:, s, :], ssap, None, ALU.mult, None, [bmo, bst], [c.b_xn[s]])
            cp(c, "pool", c.xn[:, s, 512:1024], go3[:, s, :], [bgo], [c.b_xn[s]])

    cur = loads(0)
    pre(cur)
    c.pump = None
    for it in range(n):
        (xt, bxt, xi), _, _ = cur
        nxt = loads(it + 1) if it + 1 < n else None
        wd_issue(NSTG)
        save = c.pp
        c.pp = c.pp_gu
        for s in range(2):
            xn_to_T(c, s)
        c.pp = save
        if nxt is not None:
            pre(nxt)
        yb = [[None, None], [None, None]]
        for s in range(2):
            for hh in range(2):
                ps, bps = c.pp_y[s * 2 + hh]
                yb[s][hh] = (ps, bps)
                for kc in range(KC):
                    mm(c, ps[:], c.xnT[:, kc, s * 128:(s + 1) * 128], c.wo[:, kc, hh * 512:(hh + 1) * 512],
                       kc == 0, kc == KC - 1, [c.b_xnT, c.b_wo[kc]], [bps])
        c.post_scale = 1.0
        post_residual(c, xt, bxt, yb)
        x2_bufs.append(store_tile(c, xt, bxt, xi, c.x2_s, it, "x2"))
        wd_cast()
        cur = nxt
    while wd_state["i"] < len(wd_pieces):
        wd_issue(NSTG)
        wd_cast()
    barrier(c, allb, R.bufs)
    load_gbc(c, 2)
    c.pump = ffn_load_weights(c, "ffn2", COL_G_FFN2, parts=("u",))
    next(c.pump)
    next(c.pump)
    S.final_bufs += ffn_phase(c, c.x2_s, x2_bufs, c.out, "out")
    for _ in c.pump:
        pass
    c.pump = None
def host_consts(inp):
    f = lambda k: np.asarray(inp[k], np.float32).reshape(-1)
    cs = np.zeros((128, NCONST), np.float32)
    for col, nm in ((COL_G_FFN1, "ffn1_pre_g"), (COL_G_MIX, "mix_pre_g"), (COL_G_FFN2, "ffn2_pre_g")):
        cs[:, col:col + KC] = f(nm).reshape(KC, 128).T
    cs[:, COL_G_Q:COL_G_Q + 2] = f("mla_q_norm_g").reshape(2, 128).T
    cs[:, COL_G_KV] = f("mla_kv_norm_g")
    cw = np.asarray(inp["gdn_conv_w"], np.float32).reshape(4, 1536)
    cs[:, COL_CONV:COL_CONV + 48] = cw.reshape(4, 12, 128).transpose(2, 1, 0).reshape(128, 48)
    half = 16
    freqs = (10000.0 ** (-np.arange(half, dtype=np.float32) / np.float32(half))).astype(np.float32)
    p = np.arange(128)
    cs[:, COL_FREQ] = freqs[(p % 32) % 16]
    cs[:, COL_SIGN] = np.where((p % 32) < 16, -1.0, 1.0)
    cs[:, COL_G_OUT:COL_G_OUT + 4] = f("mla_out_g").reshape(4, 128).T
    cs[:, COL_G_OUT + 4:COL_G_OUT + 8] = np.tile(f("gdn_norm_g"), 2)[:, None]
    cs[:, COL_DTB:COL_DTB + 8] = f("gdn_dt_bias")[None, :]
    cs[:, COL_ALOG:COL_ALOG + 8] = f("gdn_a_log")[None, :]
    return cs


def host_masks():
    m = np.zeros((128, 6, 128), np.float32)
    j = np.arange(128)[:, None]
    i = np.arange(128)[None, :]
    same = (j // 64) == (i // 64)
    m[:, 0] = (same & (j <= i))
    m[:, 1] = np.where(same & (j <= i), 0.0, -1.0e4)
    m[:, 2] = (same & (j < i))
    m[:, 3] = np.where(same & (i < j), 0.0, -1.0e4)
    m[:, 4] = same
    m[:, 5] = (j <= i)
    return m


def host_weights(inp):
    w_in = np.asarray(inp["w_in"], np.float32).reshape(D, 2480)
    ext = np.zeros((D, NWIN), np.float32)
    ext[:, W_CQ0:W_CQ0 + 256] = w_in[:, 0:256]
    ext[:, W_CKV0:W_CKV0 + 128] = w_in[:, 256:384]
    kpe = w_in[:, 384:416]
    ext[:, W_KPE0 + 64:W_KPE0 + 96] = kpe
    ext[:, W_KPESW0 + 64:W_KPESW0 + 96] = np.concatenate([kpe[:, 16:32], kpe[:, 0:16]], axis=1)
    ext[:, W_QKV0:W_QKV0 + 1536] = w_in[:, 416:1952]
    ext[:, W_AB0:W_AB0 + 16] = w_in[:, 1952:1968]
    ext[:, W_GATE0:W_GATE0 + 512] = w_in[:, 1968:2480]
    w_uq = np.asarray(inp["mla_w_uq"], np.float32).reshape(256, 768)
    uq = np.zeros((256, 1536), np.float32)
    for h in range(NH):
        blk = w_uq[:, h * 96:(h + 1) * 96]
        uq[:, h * 192:h * 192 + 96] = blk
        uq[:, h * 192 + 96 + 64:h * 192 + 192] = np.concatenate([blk[:, 80:96], blk[:, 64:80]], axis=1)
    w_ukv = np.asarray(inp["mla_w_ukv"], np.float32).reshape(128, 1024).reshape(128, NH, 128)
    ukv = np.concatenate([w_ukv[:, :, 0:64].reshape(128, 512), w_ukv[:, :, 64:128].reshape(128, 512)], axis=1)
    return ext, uq, np.ascontiguousarray(ukv)


def host_inputs(inputs, b, T):
    f = lambda k: np.ascontiguousarray(np.asarray(inputs[k], np.float32)[0])
    ext, uq, ukv = host_weights({k: np.asarray(v)[0] for k, v in inputs.items() if k in ("w_in", "mla_w_uq", "mla_w_ukv")})
    cin = {k: np.asarray(v)[0] for k, v in inputs.items() if k not in ("x", "positions")}
    m = {
        "x": np.ascontiguousarray(np.asarray(inputs["x"], np.float32)[b][:T]),
        "pos": np.ascontiguousarray(np.asarray(inputs["positions"], np.int32)[b][None, :T]),
        "ident": np.eye(128, dtype=np.float32),
        "masks": host_masks(),
        "consts": host_consts(cin),
        "gpost": np.stack([f("ffn1_post_g"), f("mix_post_g"), f("ffn2_post_g")]),
        "w_in_ext": ext, "w_uq_ext": uq, "w_ukv_ext": ukv, "w_out": f("w_out"),
    }
    for nm in ("ffn1", "ffn2"):
        m[nm + "_wg"] = f(nm + "_w_gate")
        m[nm + "_wu"] = f(nm + "_w_up")
        m[nm + "_wd"] = f(nm + "_w_down")
    return m


_T_FULL = 4096
_NCORES = 8


def kernel(**inputs):
    nc = build_program(_T_FULL)
    in_maps = [host_inputs(inputs, b, _T_FULL) for b in range(_NCORES)]
    for m in in_maps[1:]:
        for k in m:
            if k not in ("x", "pos"):
                m[k] = in_maps[0][k]
    res = run_bass_kernel_spmd(nc, in_maps, core_ids=list(range(_NCORES)))
    out = np.stack([np.asarray(r["out"], np.float32) for r in res.results], axis=0)
    return out
```

```python
import numpy as np
import concourse.bass as bass
import concourse.mybir as mybir
from concourse.bass_utils import run_bass_kernel_spmd

F32 = mybir.dt.float32
F32R = mybir.dt.float32r
BF16 = mybir.dt.bfloat16
I32 = mybir.dt.int32
AF = mybir.ActivationFunctionType
ALU = mybir.AluOpType
AX = mybir.AxisListType

D = 1024
DFF = 2816
NF = DFF // 128
KC = D // 128
EPS = 1e-6
TT = 256


class Buf:
    __slots__ = ("name", "w", "r", "x")

    def __init__(self, name, excl=False):
        self.name = name
        self.w = None
        self.r = {}
        self.x = excl


class Op:
    __slots__ = ("eng", "fn", "dma", "ndma", "idx", "deps", "signal", "event", "waits")


class Sched:
    def __init__(self, nc, same_engine_sync=True):
        self.nc = nc
        self.ops = []
        self.same = same_engine_sync
        self.sems = []
        self.final_bufs = []

    def new_sem(self):
        h = self.nc.alloc_semaphore(name="s%d" % len(self.sems))
        self.sems.append(h)
        return len(self.sems) - 1

    def emit(self, eng, fn, reads=(), writes=(), dma=None, ndma=1):
        op = Op()
        op.eng = eng
        op.fn = fn
        op.dma = dma
        op.ndma = ndma
        op.idx = len(self.ops)
        op.signal = False
        op.event = None
        deps = {}

        def add(d, kind):
            if d is None:
                return
            if d.dma is not None:
                need = True
            elif d.eng != eng:
                need = True
            elif eng == "pe":
                need = False
            else:
                need = self.same and kind != "war"
            if need:
                deps[d.idx] = d

        for b in reads:
            add(b.w, "raw")
            if b.x:
                for r in b.r.values():
                    if r.eng != eng:
                        add(r, "rar")
        for b in writes:
            add(b.w, "waw")
            for r in b.r.values():
                add(r, "war")
        op.deps = list(deps.values())
        for b in writes:
            b.w = op
            b.r = {}
        for b in reads:
            key = eng if dma is None else ("dma", op.idx)
            b.r[key] = op
        self.ops.append(op)
        return op

    def finalize(self):
        nc = self.nc
        ROT = 2000
        fin = self.emit("sp", None, reads=self.final_bufs)
        for op in self.ops:
            for d in op.deps:
                d.signal = True
        cnt = {}
        cur = {}
        dstate = {}
        for op in self.ops:
            if op.dma is not None:
                st = dstate.get(op.dma)
                if st is None:
                    st = [self.new_sem(), 0]
                    dstate[op.dma] = st
                st[1] += 16 * op.ndma
                op.event = (st[0], st[1])
            elif op.signal:
                e = op.eng
                if e not in cur or cnt[e] >= ROT:
                    cur[e] = self.new_sem()
                    cnt[e] = 0
                cnt[e] += 1
                op.event = (cur[e], cnt[e])
        seen = {}
        for op in self.ops:
            w = {}
            for d in op.deps:
                s, v = d.event
                if v > w.get(s, 0):
                    w[s] = v
            sn = seen.setdefault(op.eng, {})
            op.waits = [(s, v) for s, v in w.items() if v > sn.get(s, 0)]
            for s, v in op.waits:
                sn[s] = v
        sems = self.sems
        ops = self.ops

        def mk(engname):
            def run(eng):
                for op in ops:
                    if op.eng != engname:
                        continue
                    for s, v in op.waits:
                        eng.wait_ge(sems[s], v)
                    if op.fn is None:
                        continue
                    res = op.fn(eng)
                    if op.dma is not None:
                        insts = res if isinstance(res, (list, tuple)) else [res]
                        assert len(insts) == op.ndma
                        for i in insts:
                            i.then_inc(sems[op.event[0]], 16)
                    elif op.event is not None:
                        inst = res[-1] if isinstance(res, (list, tuple)) else res
                        inst.then_inc(sems[op.event[0]], 1)
            return run

        with nc.Block() as block:
            block.sync(mk("sp"))
            block.scalar(mk("act"))
            block.vector(mk("dve"))
            block.gpsimd(mk("pool"))
            block.tensor(mk("pe"))


NH = 8
W_CQ0, W_CKV0, W_KPE0, W_KPESW0, W_QKV0, W_AB0, W_GATE0, NWIN = 0, 256, 384, 480, 576, 2112, 2128, 2640
COL_G_FFN1, COL_G_MIX, COL_G_FFN2 = 0, 8, 16
COL_G_Q, COL_G_KV, COL_CONV, COL_FREQ, COL_SIGN, COL_G_OUT, COL_DTB, COL_ALOG = 24, 26, 27, 75, 76, 77, 85, 93
NCONST = 104
PIECE = 704
NSTG = 4
TWO_PI = 6.283185307179586
CW1 = 6.28125
CW2 = TWO_PI - CW1
PI_SAFE = 3.1415925


class Ctx:
    pass


class RPool:
    def __init__(self, items):
        self.items = items
        self.i = 0

    def get(self):
        it = self.items[self.i % len(self.items)]
        self.i += 1
        return it


class Region:
    def __init__(self, handle, bufs, gran_bytes, nbytes):
        self.flat = handle[:].rearrange("p a b -> p (a b)")
        self.bufs = bufs
        self.gran = gran_bytes
        self.nbytes = nbytes
        self.off = 0

    def reset(self):
        self.off = 0

    def alloc(self, free_elems, dt):
        esz = 4 if dt in (F32, I32) else 2
        nb = free_elems * esz
        self.off = (self.off + 31) // 32 * 32
        o = self.off
        assert o + nb <= self.nbytes, "region overflow"
        self.off = o + nb
        ap = self.flat[:, o // 2:(o + nb) // 2]
        if esz == 4:
            ap = ap.bitcast(dt)
        bl = self.bufs[o // self.gran:(o + nb - 1) // self.gran + 1]
        return ap, list(bl)


def wbufs(bw, kc, c0, c1):
    return [bw[kc][i] for i in range(c0 // PIECE, (c1 - 1) // PIECE + 1)]

def dma(c, out, in_, r, w, key):
    return c.S.emit("sp", (lambda e: e.dma_start(out=out, in_=in_)), reads=r, writes=w, dma=key)


def mm(c, out, lhsT, rhs, start, stop, r, w):
    return c.S.emit("pe", (lambda e: e.matmul(out, lhsT=lhsT, rhs=rhs, start=start, stop=stop)), reads=r, writes=w)


def tr(c, out, in_, ident, r, w):
    return c.S.emit("pe", (lambda e: e.transpose(out=out, in_=in_, identity=ident)), reads=r, writes=w)


def act(c, out, in_, func, r, w, **kw):
    return c.S.emit("act", (lambda e: e.activation(out=out, in_=in_, func=func, **kw)), reads=r, writes=w)


def tt(c, eng, out, in0, in1, op, r, w):
    return c.S.emit(eng, (lambda e: e.tensor_tensor(out=out, in0=in0, in1=in1, op=op)), reads=r, writes=w)


def ts(c, eng, out, in0, s1, s2, op0, op1, r, w):
    if op1 is None:
        if eng == "pool":
            return c.S.emit(eng, (lambda e: e.tensor_scalar(out=out, in0=in0, scalar1=s1, scalar2=(1.0 if op0 == ALU.mult else 0.0),
                                                            op0=op0, op1=(ALU.mult if op0 == ALU.mult else ALU.add))),
                            reads=r, writes=w)
        return c.S.emit(eng, (lambda e: e.tensor_scalar(out=out, in0=in0, scalar1=s1, scalar2=None, op0=op0)), reads=r, writes=w)
    return c.S.emit(eng, (lambda e: e.tensor_scalar(out=out, in0=in0, scalar1=s1, scalar2=s2, op0=op0, op1=op1)), reads=r, writes=w)


def stt(c, out, in0, scalar, in1, op0, op1, r, w):
    return c.S.emit("dve", (lambda e: e.scalar_tensor_tensor(out=out, in0=in0, scalar=scalar, in1=in1, op0=op0, op1=op1)),
                    reads=r, writes=w)


def cp(c, eng, out, in_, r, w):
    if eng == "act":
        return c.S.emit("act", (lambda e: e.copy(out=out, in_=in_)), reads=r, writes=w)
    return c.S.emit(eng, (lambda e: e.tensor_copy(out=out, in_=in_)), reads=r, writes=w)


def barrier(c, old, new):
    c.S.emit("pool", (lambda e: e.memset(c.dummy[:], 0.0)), reads=list(old), writes=list(old) + list(new) + [c.b_dummy])


def cast_scaled(c, out_ap, in_ap, scale_ap, reads, writes):
    i = c.cast_i
    c.cast_i += 1
    engs = c.cast_engs
    which = engs[i % len(engs)]
    if scale_ap is None:
        cp(c, which, out_ap, in_ap, reads, writes)
    elif which == "act":
        act(c, out_ap, in_ap, AF.Copy, reads, writes, scale=scale_ap)
    else:
        ts(c, which, out_ap, in_ap, scale_ap, None, ALU.mult, None, reads, writes)


def load_cast(c, dst, src, ncols, scale_ap, wb_fn, extra_reads=()):
    c0 = 0
    while c0 < ncols:
        c1 = min(ncols, c0 + PIECE)
        si = c.stg_i % NSTG
        c.stg_i += 1
        stg, bst = c.stg[si], c.b_stg[si]
        dma(c, stg[:, 0:c1 - c0], src[:, c0:c1], [], [bst], "stg%d" % si)
        cast_scaled(c, dst[:, c0:c1], stg[:, 0:c1 - c0], scale_ap, [bst, c.b_consts] + list(extra_reads), wb_fn(c0, c1))
        c0 = c1


def load_cast_piece(c, dst, src, scale_ap, wbufs_, extra_reads=()):
    si = c.stg_i % NSTG
    c.stg_i += 1
    stg, bst = c.stg[si], c.b_stg[si]
    n = dst.shape[1]
    dma(c, stg[:, 0:n], src, [], [bst], "stg%d" % si)
    cast_scaled(c, dst, stg[:, 0:n], scale_ap, [bst, c.b_consts] + list(extra_reads), wbufs_)


def ffn_load_weights(c, nm, gcol, parts=("gu", "d")):
    mats = {"g": (c.wg, c.b_wg, c.w[nm + "_g"]), "u": (c.wu, c.b_wu, c.w[nm + "_u"])}
    npc = DFF // PIECE
    for part in parts:
        if part == "d":
            wd = c.w[nm + "_d"]
            for f in range(NF):
                load_cast(c, c.wd[:, f, :], wd[f * 128:(f + 1) * 128, :], D, None, (lambda a, b, f=f: [c.b_wd[f]]))
                yield
            continue
        order = [(m, kc, p) for p in range(npc) for kc in range(KC) for m in part]
        for i, (m, kc, p) in enumerate(order):
            wsb, bufs, wd = mats[m]
            load_cast_piece(c, wsb[:, kc, p * PIECE:(p + 1) * PIECE], wd[kc * 128:(kc + 1) * 128, p * PIECE:(p + 1) * PIECE],
                            c.consts[:, gcol + kc:gcol + kc + 1], [bufs[kc][p]])
            if i % 4 == 3:
                yield


def rstd_from_ss(c, ss_ap, out_ap, n, post_scale, bst):
    p2 = float(post_scale) ** 2
    ts(c, "dve", out_ap, ss_ap, 1.0 / (n * p2), EPS / p2, ALU.mult, ALU.add, [bst], [bst])
    ncol = out_ap.shape[1]
    ex = c.cb[:, 4:5] if ncol == 1 else c.cb[:, 4:5].to_broadcast([128, ncol])
    tt(c, "pool", out_ap, out_ap, ex, ALU.pow, [bst, c.b_consts], [bst])


def load_tile(c, src, it, srcbufs=()):
    t0 = it * TT
    xi = c.xt_i % 2
    c.xt_i += 1
    xt, bxt = c.xt[xi], c.b_xt[xi]
    srcv = src[t0:t0 + TT, :].rearrange("(s p) d -> p s d", p=128)
    dma(c, xt[:], srcv, list(srcbufs), [bxt], "xt%d" % xi)
    return xt, bxt, xi


def prenorm_T(c, xt, bxt):
    for s in range(2):
        bst = c.b_st[s]
        ssap = c.st[:, s:s + 1]
        act(c, c.xn[:, s, :], xt[:, s, :], AF.Square, [bxt], [c.b_xn[s], bst], accum_out=ssap)
        rstd_from_ss(c, ssap, ssap, D, 1.0, bst)
        ts(c, "dve", c.xn[:, s, :], xt[:, s, :], ssap, None, ALU.mult, None, [bxt, bst], [c.b_xn[s]])
        xn_to_T(c, s)


def xn_to_T(c, s):
    pst, bps = c.pp.get()
    pstb = pst.bitcast(BF16)
    for kc in range(KC):
        tr(c, pstb[:, kc * 128:(kc + 1) * 128], c.xn[:, s, kc * 128:(kc + 1) * 128], c.ident_b[:], [c.b_xn[s], c.b_ident], [bps])
    cp(c, "dve", c.xnT[:, :, s * 128:(s + 1) * 128], pstb.rearrange("p (k t) -> p k t", k=KC), [bps], [c.b_xnT])


def post_residual(c, xt, bxt, ybanks):
    for s in range(2):
        bst = c.b_st[2 + s]
        for hh in range(2):
            ps, bps = ybanks[s][hh]
            col = 4 + s * 2 + hh
            act(c, c.junk[:, 0:512], ps[:], AF.Square, [bps], [c.b_junk, bst], accum_out=c.st[:, col:col + 1])
        rcol = 8 + s
        rap = c.st[:, rcol:rcol + 1]
        tt(c, "dve", rap, c.st[:, 4 + 2 * s:5 + 2 * s], c.st[:, 5 + 2 * s:6 + 2 * s], ALU.add, [bst], [bst])
        rstd_from_ss(c, rap, rap, D, c.post_scale, bst)
        for hh in range(2):
            ps, bps = ybanks[s][hh]
            ti = (s * 2 + hh) % 2
            tmp, btmp = c.tmp[ti], c.b_tmp[ti]
            gbc, b_gbc = c.gbc_use
            stt(c, tmp[:], ps[:], rap, gbc[:, hh * 512:(hh + 1) * 512], ALU.mult, ALU.mult, [bps, bst, b_gbc], [btmp])
            tt(c, "pool", xt[:, s, hh * 512:(hh + 1) * 512], xt[:, s, hh * 512:(hh + 1) * 512], tmp[:], ALU.add,
               [bxt, btmp], [bxt])


def load_gbc(c, row):
    dma(c, c.gbc[:], c.gpost_d[row:row + 1, :].partition_broadcast(128), [], [c.b_gbc], "gbc")


def ffn_pre(c, xt, bxt):
    save = c.pp
    c.pp = c.pp_gu
    prenorm_T(c, xt, bxt)
    c.pp = save


def ffn_gu(c):
    for f in range(NF):
        if c.pump is not None:
            next(c.pump, None)
            next(c.pump, None)
        ps, bps = c.pp_gu.get()
        for m, (wsb, bw) in enumerate(((c.wg, c.b_wg), (c.wu, c.b_wu))):
            for kc in range(KC):
                mm(c, ps[:, m * TT:(m + 1) * TT], wsb[:, kc, f * 128:(f + 1) * 128], c.xnT[:, kc, :],
                   kc == 0, kc == KC - 1, wbufs(bw, kc, f * 128, (f + 1) * 128) + [c.b_xnT], [bps])
        sgi = f % 2
        sg, bsg = c.sg[sgi], c.b_sg[sgi]
        act(c, sg[:], ps[:, 0:TT], AF.Silu, [bps], [bsg])
        tt(c, "dve", c.hT[:, f, :], sg[:], ps[:, TT:2 * TT], ALU.mult, [bps, bsg], [c.b_hT[f]])


def ffn_down(c, xt, bxt):
    yb = [[None, None], [None, None]]
    for s in range(2):
        for hh in range(2):
            ps, bps = c.pp_y[s * 2 + hh]
            yb[s][hh] = (ps, bps)
            for f in range(NF):
                mm(c, ps[:], c.hT[:, f, s * 128:(s + 1) * 128], c.wd[:, f, hh * 512:(hh + 1) * 512],
                   f == 0, f == NF - 1, [c.b_hT[f], c.b_wd[f]], [bps])
    c.post_scale = 0.5
    post_residual(c, xt, bxt, yb)


def ffn_core(c, xt, bxt):
    ffn_pre(c, xt, bxt)
    ffn_gu(c)
    ffn_down(c, xt, bxt)


def store_tile(c, xt, bxt, xi, dst, it, name):
    t0 = it * TT
    dstv = dst[t0:t0 + TT, :].rearrange("(s p) d -> p s d", p=128)
    bd = Buf("%s_%d" % (name, it))
    dma(c, dstv, xt[:], [bxt], [bd], "st_xt%d_%s" % (xi, name))
    return bd

def load_wo_row(c):
    kc = c.wo_next
    if kc >= KC:
        return
    c.wo_next += 1
    load_cast(c, c.wo[:, kc, :], c.w_out_d[kc * 128:(kc + 1) * 128, :], D,
              c.consts[:, COL_G_OUT + kc:COL_G_OUT + kc + 1], (lambda a, b, kc=kc: [c.b_wo[kc]]))


def ffn_phase(c, src, srcbufs, dst, name):
    T = c.T
    n = T // TT
    sb = (lambda it: [srcbufs[it]]) if srcbufs is not None else (lambda it: [])
    outb = []
    nxt = load_tile(c, src, 0, sb(0))
    ffn_pre(c, nxt[0], nxt[1])
    for it in range(n):
        xt, bxt, xi = nxt
        if it + 1 < n:
            nxt = load_tile(c, src, it + 1, sb(it + 1))
        ffn_gu(c)
        if it + 1 < n:
            ffn_pre(c, nxt[0], nxt[1])
        ffn_down(c, xt, bxt)
        outb.append(store_tile(c, xt, bxt, xi, dst, it, name))

    return outb


def build_program(T, debug=(), upto=9, p4_stop=99):
    nc = bass.Bass("TRN2", target_bir_lowering=False)
    S = Sched(nc)
    c = Ctx()
    c.nc, c.S, c.T = nc, S, T
    c.p4_stop = p4_stop
    c.p4_serial = (p4_stop == 77)
    NT = T // 128
    c.NT = NT

    def din(name, shape, dt=F32):
        return nc.dram_tensor(name, list(shape), dt, kind="ExternalInput").ap()

    def dscr(name, shape, dt):
        kind = "ExternalOutput" if name in debug else "Internal"
        return nc.dram_tensor(name, list(shape), dt, kind=kind).ap()

    c.x = din("x", [T, D])
    c.pos_d = din("pos", [1, T], I32)
    c.ident_d = din("ident", [128, 128])
    c.masks_d = din("masks", [128, 6, 128])
    c.consts_d = din("consts", [128, NCONST])
    c.gpost_d = din("gpost", [3, D])
    c.w = {}
    for nm in ("ffn1", "ffn2"):
        c.w[nm + "_g"] = din(nm + "_wg", [D, DFF])
        c.w[nm + "_u"] = din(nm + "_wu", [D, DFF])
        c.w[nm + "_d"] = din(nm + "_wd", [DFF, D])
    c.w_in_d = din("w_in_ext", [D, NWIN])
    c.w_uq_d = din("w_uq_ext", [256, 1536])
    c.w_ukv_d = din("w_ukv_ext", [128, 1024])
    c.w_out_d = din("w_out", [D, D])
    c.out = nc.dram_tensor("out", [T, D], F32, kind="ExternalOutput").ap()
    c.x1_s = dscr("x1_s", [T, D], F32)
    c.qT_s = dscr("qT_s", [NH, 96, T], BF16)
    c.knope_s = dscr("knope_s", [NH, 64, T], BF16)
    c.kpe_s = dscr("kpe_s", [32, T], BF16)
    c.v_s = dscr("v_s", [128, NT, NH * 64], BF16)
    c.gq_s = dscr("gq_s", [4, 128, T], BF16)
    c.gk_s = dscr("gk_s", [4, 128, T], BF16)
    c.kvtok_s = dscr("kvtok_s", [T, 1024], BF16)
    c.gb_s = dscr("gb_s", [T, 16], F32)
    c.gbT_s = dscr("gbT_s", [NT, 16, 128], F32)
    c.gate_s = dscr("gate_s", [T, 512], F32)
    c.x2_s = dscr("x2_s", [T, D], F32)
    c.mla_s = dscr("mla_s", [T, 512], F32)
    c.gdn_s = dscr("gdn_s", [T, 512], F32)

    def sb(name, shape, dt=F32):
        return nc.alloc_sbuf_tensor(name, list(shape), dt)

    c.ident_f = sb("ident_f", [128, 128])
    c.ident_b = sb("ident_b", [128, 128], BF16)
    c.consts = sb("consts_sb", [128, NCONST])
    c.cb = sb("cb", [128, 8])
    c.dummy = sb("dummy_t", [128, 1])
    c.gbc = sb("gbc", [128, D])
    c.wg = sb("wg", [128, KC, DFF], BF16)
    c.wu = sb("wu", [128, KC, DFF], BF16)
    c.wd = sb("wd", [128, NF, D], BF16)
    c.wo = sb("wo", [128, KC, D], BF16)
    c.stg = [sb("stg%d" % i, [128, PIECE]) for i in range(NSTG)]
    c.xt = [sb("xt%d" % i, [128, 2, D]) for i in range(2)]
    c.xn = sb("xn", [128, 2, D], BF16)
    c.xnT = sb("xnT", [128, KC, TT], BF16)
    c.hT = sb("hT", [128, NF, TT], BF16)
    c.sg = [sb("sg%d" % i, [128, TT]) for i in range(2)]
    c.tmp = [sb("tmpy%d" % i, [128, 512]) for i in range(2)]
    c.junk = sb("junk", [128, 512], BF16)
    c.st = sb("stats", [128, 16])
    c.ps = [nc.alloc_psum_tensor("ps%d" % i, [128, 512], F32) for i in range(8)]

    c.b_ident = Buf("ident")
    c.b_consts = Buf("consts")
    c.b_dummy = Buf("dummy")
    c.b_gbc = Buf("gbc")
    c.gbc_use = (c.gbc, c.b_gbc)
    c.cast_engs = ("dve", "pool", "act")
    npc = DFF // PIECE
    c.b_wg = [[Buf("wg%d_%d" % (k, h)) for h in range(npc)] for k in range(KC)]
    c.b_wu = [[Buf("wu%d_%d" % (k, h)) for h in range(npc)] for k in range(KC)]
    c.b_wd = [Buf("wd%d" % f) for f in range(NF)]
    c.b_wo = [Buf("wo%d" % k) for k in range(KC)]
    c.b_stg = [Buf("stg%d" % i) for i in range(NSTG)]
    c.b_xt = [Buf("xt%d" % i) for i in range(2)]
    c.b_xn = [Buf("xn%d" % i) for i in range(2)]
    c.b_xnT = Buf("xnT")
    c.b_hT = [Buf("hT%d" % f) for f in range(NF)]
    c.b_sg = [Buf("sg%d" % i) for i in range(2)]
    c.b_tmp = [Buf("tmp%d" % i) for i in range(2)]
    c.b_junk = Buf("junk")
    c.b_st = [Buf("st%d" % i) for i in range(16)]
    c.b_ps = [Buf("ps%d" % i, excl=True) for i in range(8)]
    c.stg_i = 0
    c.wo_next = 0
    c.cast_i = 0
    c.xt_i = 0
    allps = [(c.ps[i], c.b_ps[i]) for i in range(8)]
    c.pp = RPool(allps)
    c.pp_gu = RPool(allps[0:4])
    c.pp_y = allps[4:8]
    c.reg_u = Region(c.wu, [b for row in c.b_wu for b in row], PIECE * 2, KC * DFF * 2)
    c.reg_d = Region(c.wd, c.b_wd, D * 2, NF * D * 2)

    dma(c, c.ident_f[:], c.ident_d, [], [c.b_ident], "ident")
    dma(c, c.consts[:], c.consts_d, [], [c.b_consts], "consts")
    cp(c, "dve", c.ident_b[:], c.ident_f[:], [c.b_ident], [c.b_ident])
    for i, v in enumerate((EPS, float(np.log(0.125)), 1.0, 0.0, -0.5)):
        c.S.emit("pool", (lambda e, i=i, v=v: e.memset(c.cb[:, i:i + 1], v)), writes=[c.b_consts])

    c.pump = ffn_load_weights(c, "ffn1", COL_G_FFN1)
    for _ in range(4):
        next(c.pump)
    load_gbc(c, 0)
    c.x1_bufs = ffn_phase(c, c.x, None, c.x1_s, "x1")
    for _ in c.pump:
        pass
    c.pump = None
    c.dbg_final = list(c.x1_bufs)
    if upto >= 2:
        phase2(c)
    pump = ffn_load_weights(c, "ffn2", COL_G_FFN2, parts=("g",))
    c.cast_engs = ("dve", "pool")
    if upto >= 3:
        phase3(c, pump)
    if upto >= 4:
        c.op_on = False
        if c.op_on:
            outproj_setup(c)
        phase4(c)
    c.cast_engs = ("dve", "pool", "act")
    if upto >= 5:
        phase5(c, pump)
    S.final_bufs = S.final_bufs + c.dbg_final
    S.finalize()
    return nc


def phase2(c):
    S, T, NT = c.S, c.T, c.NT
    R = c.reg_u
    R.reset()
    R2 = c.reg_d
    R2.reset()
    allb = []

    def alloc(n, dt, reg=None):
        ap, _ = (reg or R).alloc(n, dt)
        b = Buf("p2_%d" % len(allb))
        allb.append(b)
        return ap, b

    wuq, b_wuq = alloc(2 * 1536, BF16)
    wuq = wuq.rearrange("p (k n) -> p k n", k=2)
    wukv, b_wukv = alloc(1024, BF16)
    negA, b_negA = alloc(8, F32)
    blk, b_blk = alloc(128, F32)
    mui, b_mui = alloc(128, F32, R2)
    cmb, b_cmb = alloc(2 * 16, F32, R2)
    cmb = cmb.rearrange("p (s n) -> p s n", s=2)
    gbT, b_gbT = alloc(2 * 128, F32, R2)
    gbT = gbT.rearrange("p (s n) -> p s n", s=2)
    hal = [alloc(4, F32) for _ in range(12)]
    posi, b_posi = alloc(TT, I32)
    kfi, b_kfi = alloc(2 * TT, I32)
    rA, b_rA = alloc(2 * TT, F32)
    rB, b_rB = alloc(2 * TT, F32)
    rC, b_rC = alloc(2 * TT, F32)
    tabs = [alloc(2 * TT, F32) for _ in range(2)]
    cqn, b_cqn = alloc(2 * 384, BF16)
    cqn = cqn.rearrange("p (s n) -> p s n", s=2)
    cqnT, b_cqnT = alloc(3 * TT, BF16)
    cqnT = cqnT.rearrange("p (k t) -> p k t", k=3)
    z8, b_z8 = alloc(32, F32)
    gb, b_gb = alloc(2 * 16, F32)
    gb = gb.rearrange("p (s n) -> p s n", s=2)
    gate_sb, b_gate = alloc(2 * 512, F32, R2)
    gate_sb = gate_sb.rearrange("p (s n) -> p s n", s=2)
    ktp = RPool([(alloc(TT, F32), alloc(TT, F32)) for _ in range(2)])
    kpe_rot, b_kpe = alloc(TT, BF16)
    xcp = RPool([alloc(TT + 4, F32) for _ in range(3)])
    accp = RPool([alloc(TT, F32) for _ in range(3)])
    ys = [alloc(TT, F32, R2) for _ in range(8)]
    sqp = RPool([alloc(TT, F32, R2) for _ in range(4)])
    featp = RPool([alloc(TT, BF16, R2) for _ in range(4)])
    tok, b_tok = alloc(2 * 1024, BF16, R2)
    tok = tok.rearrange("p (s n) -> p s n", s=2)
    qp = RPool([alloc(TT, BF16, R2) for _ in range(3)])
    knp = RPool([alloc(TT, BF16, R2) for _ in range(3)])
    v_sb, b_vsb = alloc(2 * 512, BF16, R2)
    v_sb = v_sb.rearrange("p (s n) -> p s n", s=2)

    barrier(c, R.bufs + R2.bufs, allb)
    for kc in range(KC):
        load_cast(c, c.wg[:, kc, 0:NWIN], c.w_in_d[kc * 128:(kc + 1) * 128, :], NWIN,
                  c.consts[:, COL_G_MIX + kc:COL_G_MIX + kc + 1], (lambda a, b, kc=kc: wbufs(c.b_wg, kc, a, b)))
    for kc in range(2):
        load_cast(c, wuq[:, kc, :], c.w_uq_d[kc * 128:(kc + 1) * 128, :], 1536,
                  c.consts[:, COL_G_Q + kc:COL_G_Q + kc + 1], (lambda a, b: [b_wuq]))
    load_cast(c, wukv, c.w_ukv_d[:, :], 1024, c.consts[:, COL_G_KV:COL_G_KV + 1], (lambda a, b: [b_wukv]))
    dma(c, blk, c.masks_d[:, 4, :], [], [b_blk], "blk")
    dma(c, mui, c.masks_d[:, 0, :], [], [b_mui], "mui")
    act(c, negA, c.consts[:, COL_ALOG:COL_ALOG + 8], AF.Exp, [c.b_consts], [b_negA])
    ts(c, "dve", negA, negA, -1.0, None, ALU.mult, None, [b_negA], [b_negA])
    for ch in range(12):
        S.emit("pool", (lambda e, ch=ch: e.memset(hal[ch][0], 0.0)), writes=[hal[ch][1]])

    gq_b, gk_b, kvtok_b, gb_b, gate_b, qT_b, kn_b, kpe_b, v_b, gbT_b = [], [], [], [], [], [], [], [], [], []
    rA3 = rA.rearrange("p (a t) -> p a t", a=2)
    def rope_args(it):
        t0 = it * TT
        dma(c, posi, c.pos_d[0:1, t0:t0 + TT].partition_broadcast(128), [], [b_posi], "posi")
        yield
        cp(c, "pool", rA[:, 0:TT], posi, [b_posi], [b_rA])
        yield
        ts(c, "pool", rA[:, 0:TT], rA[:, 0:TT], c.consts[:, COL_FREQ:COL_FREQ + 1], None, ALU.mult, None, [b_rA, c.b_consts], [b_rA])
        yield
        ts(c, "pool", rA[:, TT:2 * TT], rA[:, 0:TT], float(np.pi / 2), None, ALU.add, None, [b_rA], [b_rA])
        yield
        ts(c, "pool", rB, rA, float(1.0 / TWO_PI), None, ALU.mult, None, [b_rA], [b_rB])
        yield
        cp(c, "pool", kfi, rB, [b_rB], [b_kfi])
        yield
        cp(c, "pool", rB, kfi, [b_kfi], [b_rB])
        yield
        ts(c, "pool", rC, rB, -CW1, None, ALU.mult, None, [b_rB], [b_rC])
        yield
        tt(c, "pool", rC, rC, rA, ALU.add, [b_rC, b_rA], [b_rC])
        yield
        ts(c, "pool", rB, rB, -CW2, None, ALU.mult, None, [b_rB], [b_rB])
        yield
        tt(c, "pool", rC, rC, rB, ALU.add, [b_rC, b_rB], [b_rC])
        yield
        ts(c, "dve", rB, rC, float(np.pi), -TWO_PI, ALU.is_gt, ALU.mult, [b_rC], [b_rB])
        yield
        tt(c, "pool", rC, rC, rB, ALU.add, [b_rC, b_rB], [b_rC])
        yield
        ts(c, "dve", rB, rC, float(-np.pi), TWO_PI, ALU.is_lt, ALU.mult, [b_rC], [b_rB])
        yield
        tt(c, "pool", rC, rC, rB, ALU.add, [b_rC, b_rB], [b_rC])
        yield
        ts(c, "dve", rC, rC, PI_SAFE, -PI_SAFE, ALU.min, ALU.max, [b_rC], [b_rC])
        yield

    def rope_table(it):
        tab, b_tab = tabs[it % 2]
        act(c, tab, rC, AF.Sin, [b_rC], [b_tab])
        ts(c, "pool", tab[:, 0:TT], tab[:, 0:TT], c.consts[:, COL_SIGN:COL_SIGN + 1], None, ALU.mult, None, [b_tab, c.b_consts], [b_tab])

    for _ in rope_args(0):
        pass
    rope_table(0)
    nxt_tile = load_tile(c, c.x1_s, 0, [c.x1_bufs[0]])
    for it in range(T // TT):
        t0 = it * TT
        if it == 0:
            xt, bxt, xi = nxt_tile
            prenorm_T(c, xt, bxt)
        if it + 1 < T // TT:
            nxt_tile = load_tile(c, c.x1_s, it + 1, [c.x1_bufs[it + 1]])
        deferred = []
        late = []
        tm = []
        for s in range(2):
            lhs = lambda kc: c.xnT[:, kc, s * 128:(s + 1) * 128]
            psA, bA = c.pp.get()
            for kc in range(KC):
                mm(c, psA[:, 0:384], lhs(kc), c.wg[:, kc, 0:384], kc == 0, kc == KC - 1,
                   [c.b_xnT] + wbufs(c.b_wg, kc, 0, 384), [bA])
            psB, bB = c.pp.get()
            for kc in range(KC):
                mm(c, psB[:, 0:16], lhs(kc), c.wg[:, kc, W_AB0:W_AB0 + 16], kc == 0, kc == KC - 1,
                   [c.b_xnT] + wbufs(c.b_wg, kc, W_AB0, W_AB0 + 16), [bB])
            psC, bC = c.pp.get()
            for kc in range(KC):
                mm(c, psC[:, 0:512], lhs(kc), c.wg[:, kc, W_GATE0:W_GATE0 + 512], kc == 0, kc == KC - 1,
                   [c.b_xnT] + wbufs(c.b_wg, kc, W_GATE0, W_GATE0 + 512), [bC])
            tm.append((psA, bA, psB, bB, psC, bC))
        for s in range(2):
            psA, bA, psB, bB, psC, bC = tm[s]
            bst = c.b_st[10 + s]
            act(c, c.junk[:, 0:256], psA[:, 0:256], AF.Square, [bA], [c.b_junk, bst], accum_out=c.st[:, 10 + s:11 + s])
            rstd_from_ss(c, c.st[:, 10 + s:11 + s], c.st[:, 10 + s:11 + s], 256, 1.0, bst)
            ts(c, "dve", cqn[:, s, 0:256], psA[:, 0:256], c.st[:, 10 + s:11 + s], None, ALU.mult, None, [bA, bst], [b_cqn])
            bst2 = c.b_st[12 + s]
            act(c, c.junk[:, 256:384], psA[:, 256:384], AF.Square, [bA], [c.b_junk, bst2], accum_out=c.st[:, 12 + s:13 + s])
            rstd_from_ss(c, c.st[:, 12 + s:13 + s], c.st[:, 12 + s:13 + s], 128, 1.0, bst2)
            ts(c, "dve", cqn[:, s, 256:384], psA[:, 256:384], c.st[:, 12 + s:13 + s], None, ALU.mult, None, [bA, bst2], [b_cqn])
            def cq_T(s=s):
                psT, bT = c.pp.get()
                psTb = psT.bitcast(BF16)
                for i in range(3):
                    tr(c, psTb[:, i * 128:(i + 1) * 128], cqn[:, s, i * 128:(i + 1) * 128], c.ident_b[:], [b_cqn, c.b_ident], [bT])
                cp(c, "dve", cqnT[:, :, s * 128:(s + 1) * 128], psTb[:, 0:384].rearrange("p (k t) -> p k t", k=3), [bT], [b_cqnT])

            late.append(cq_T)
            zz = z8[:, s * 16:s * 16 + 8]
            tt(c, "dve", zz, psB[:, 0:8], c.consts[:, COL_DTB:COL_DTB + 8], ALU.add, [bB, c.b_consts], [b_z8])
            act(c, zz, zz, AF.Exp, [b_z8], [b_z8])
            act(c, zz, zz, AF.Ln, [b_z8, c.b_consts], [b_z8], bias=c.cb[:, 2:3])
            tt(c, "dve", gb[:, s, 0:8], zz, negA, ALU.mult, [b_z8, b_negA], [b_gb])
            deferred.append((s, psB, bB, psC, bC))
        tab, b_tab = tabs[it % 2]
        sin2, cos2 = tab[:, 0:TT], tab[:, TT:2 * TT]
        for s, psB, bB, psC, bC in deferred:
            act(c, gate_sb[:, s, :], psC[:, 0:512], AF.Silu, [bC], [b_gate])
            zt = z8[:, s * 16 + 8:s * 16 + 16]
            act(c, zt, psB[:, 8:16], AF.Tanh, [bB], [b_z8], scale=0.5)
            ts(c, "dve", gb[:, s, 8:16], zt, 0.5, 0.5, ALU.mult, ALU.add, [b_z8], [b_gb])
        bd = Buf("gb_s%d" % it)
        dma(c, c.gb_s[t0:t0 + TT, :].rearrange("(s p) n -> p s n", p=128), gb, [b_gb], [bd], "st_gb")
        gb_b.append(bd)
        def rope_rows(ps1, b1, ps2, b2, out_ap, out_b):
            (kt1, b_kt1), (kt2, b_kt2) = ktp.get()
            tt(c, "dve", kt1[64:96, :], ps1[64:96, 0:TT], cos2[64:96, :], ALU.mult, [b1, b_tab], [b_kt1])
            tt(c, "dve", kt2[64:96, :], ps2[64:96, 0:TT], sin2[64:96, :], ALU.mult, [b2, b_tab], [b_kt2])
            tt(c, "pool", out_ap[64:96, :], kt1[64:96, :], kt2[64:96, :], ALU.add, [b_kt1, b_kt2], [out_b])

        ps1, b1 = c.pp.get()
        ps2, b2 = c.pp.get()
        for kc in range(KC):
            mm(c, ps1[0:96, 0:TT], c.wg[:, kc, W_KPE0:W_KPE0 + 96], c.xnT[:, kc, :], kc == 0, kc == KC - 1,
               [c.b_xnT] + wbufs(c.b_wg, kc, W_KPE0, W_KPE0 + 96), [b1])
        for kc in range(KC):
            mm(c, ps2[0:96, 0:TT], c.wg[:, kc, W_KPESW0:W_KPESW0 + 96], c.xnT[:, kc, :], kc == 0, kc == KC - 1,
               [c.b_xnT] + wbufs(c.b_wg, kc, W_KPESW0, W_KPESW0 + 96), [b2])
        rope_rows(ps1, b1, ps2, b2, kpe_rot, b_kpe)
        bd = Buf("kpe_s%d" % it)
        dma(c, c.kpe_s[:, t0:t0 + TT], kpe_rot[64:96, :], [b_kpe], [bd], "st_kpe")
        kpe_b.append(bd)
        for fn_ in late:
            fn_()

        def conv_mm(ch):
            ps, bp = c.pp.get()
            c0 = W_QKV0 + ch * 128
            for kc in range(KC):
                mm(c, ps[:, 0:TT], c.wg[:, kc, c0:c0 + 128], c.xnT[:, kc, :], kc == 0, kc == KC - 1,
                   [c.b_xnT] + wbufs(c.b_wg, kc, c0, c0 + 128), [bp])
            return ps, bp

        def conv_x(ch, ps, bp):
            xc, bxc = xcp.get()
            hl, bhl = hal[ch]
            cp(c, "act", xc[:, 3:3 + TT], ps[:, 0:TT], [bp], [bxc])
            cp(c, "act", xc[:, 0:3], hl[:, 0:3], [bhl], [bxc])
            cp(c, "act", hl[:, 0:3], xc[:, TT:TT + 3], [bxc], [bhl])
            acc, bacc = accp.get()
            act(c, acc, ps[:, 0:TT], AF.Copy, [bp, c.b_consts], [bacc],
                scale=c.consts[:, COL_CONV + ch * 4 + 3:COL_CONV + ch * 4 + 4])
            return xc, bxc, acc, bacc

        def conv_post(ch, xc, bxc, acc, bacc):
            wcol = lambda j: c.consts[:, COL_CONV + ch * 4 + j:COL_CONV + ch * 4 + j + 1]
            for j in (2, 1, 0):
                stt(c, acc, xc[:, j:j + TT], wcol(j), acc, ALU.mult, ALU.add, [bxc, bacc, c.b_consts], [bacc])
            if ch < 8:
                y, by = ys[ch]
                act(c, y, acc, AF.Silu, [bacc], [by])
            else:
                feat, bfeat = featp.get()
                act(c, feat, acc, AF.Silu, [bacc], [bfeat])
                to_tok(ch, feat, bfeat, "act")

        def to_tok(ch, feat, bfeat, eng="dve"):
            pst, bt = c.pp.get()
            pstb = pst.bitcast(BF16)
            for s in range(2):
                tr(c, pstb[:, s * 128:(s + 1) * 128], feat[:, s * 128:(s + 1) * 128], c.ident_b[:], [bfeat, c.b_ident], [bt])
            cc = (ch - 4) * 128
            cp(c, eng, tok[:, :, cc:cc + 128], pstb[:, 0:256].rearrange("p (s n) -> p s n", s=2), [bt], [b_tok])

        def mla_head(h):
            ps1, b1 = c.pp.get()
            ps2, b2 = c.pp.get()
            for kc in range(2):
                mm(c, ps1[0:96, 0:TT], wuq[:, kc, h * 192:h * 192 + 96], cqnT[:, kc, :], kc == 0, kc == 1, [b_wuq, b_cqnT], [b1])
            for kc in range(2):
                mm(c, ps2[0:96, 0:TT], wuq[:, kc, h * 192 + 96:h * 192 + 192], cqnT[:, kc, :], kc == 0, kc == 1, [b_wuq, b_cqnT], [b2])
            ps3, b3 = c.pp.get()
            mm(c, ps3[0:64, 0:TT], wukv[:, h * 64:(h + 1) * 64], cqnT[:, 2, :], True, True, [b_wukv, b_cqnT], [b3])
            q_sb, bq = qp.get()
            cp(c, "act", q_sb[0:64, :], ps1[0:64, 0:TT], [b1], [bq])
            rope_rows(ps1, b1, ps2, b2, q_sb, bq)
            bd = Buf("qT_s%d_%d" % (h, it))
            dma(c, c.qT_s[h][:, t0:t0 + TT], q_sb[0:96, :], [bq], [bd], "st_q%d" % ((qp.i - 1) % 3))
            qT_b.append(bd)
            kn, bkn = knp.get()
            cp(c, "act", kn[0:64, :], ps3[0:64, 0:TT], [b3], [bkn])
            bd = Buf("kn_s%d_%d" % (h, it))
            dma(c, c.knope_s[h][:, t0:t0 + TT], kn[0:64, :], [bkn], [bd], "st_kn%d" % ((knp.i - 1) % 3))
            kn_b.append(bd)

        rg = rope_args(it + 1) if it + 1 < T // TT else iter(())
        pm = {0: conv_mm(0), 1: conv_mm(1)}
        px = {0: conv_x(0, *pm.pop(0))}
        for ch in range(12):
            if ch + 2 < 12:
                pm[ch + 2] = conv_mm(ch + 2)
            if ch + 1 < 12:
                px[ch + 1] = conv_x(ch + 1, *pm.pop(ch + 1))
            conv_post(ch, *px.pop(ch))
            next(rg, None)
            next(rg, None)
            if ch < 8:
                mla_head(ch)
        for _ in rg:
            pass
        if it + 1 < T // TT:
            rope_table(it + 1)
            xt, bxt, xi = nxt_tile
            prenorm_T(c, xt, bxt)
        for s in range(2):
            ps4, b4 = c.pp.get()
            mm(c, ps4[:, 0:512], cqnT[:, 2, s * 128:(s + 1) * 128], wukv[:, 512:1024], True, True, [b_wukv, b_cqnT], [b4])
            cp(c, "act", v_sb[:, s, :], ps4[:, 0:512], [b4], [b_vsb])
        bd = Buf("v_s%d" % it)
        dma(c, c.v_s[:, 2 * it:2 * it + 2, :], v_sb, [b_vsb], [bd], "st_v")
        v_b.append(bd)
        def norm_a(ch):
            y, by = ys[ch]
            sq, bsq = sqp.get()
            tt(c, "pool", sq, y, y, ALU.mult, [by], [bsq])
            psn, bn = c.pp.get()
            mm(c, psn[:, 0:TT], blk, sq, True, True, [b_blk, bsq], [bn])
            return sq, bsq, psn, bn

        pn = {0: norm_a(0), 1: norm_a(1)}
        for ch in range(8):
            if ch + 2 < 8:
                pn[ch + 2] = norm_a(ch + 2)
            y, by = ys[ch]
            sq, bsq, psn, bn = pn.pop(ch)
            act(c, sq, psn[:, 0:TT], AF.Ln, [bn, c.b_consts], [bsq], bias=c.cb[:, 0:1])
            act(c, sq, sq, AF.Exp, [bsq, c.b_consts], [bsq], scale=-0.5, bias=(c.cb[:, 1:2] if ch < 4 else c.cb[:, 3:4]))
            feat, bfeat = featp.get()
            tt(c, "dve", feat, y, sq, ALU.mult, [by, bsq], [bfeat])
            dst, lst = (c.gq_s, gq_b) if ch < 4 else (c.gk_s, gk_b)
            bd = Buf("g%d_%d" % (ch, it))
            dma(c, dst[ch % 4][:, t0:t0 + TT], feat, [bfeat], [bd], "st_feat%d_%d" % ((featp.i - 1) % 4, ch // 4))
            lst.append(bd)
            if ch >= 4:
                to_tok(ch, feat, bfeat)
        for s in range(2):
            psg, bg_ = c.pp.get()
            mm(c, psg[:, 0:8], mui, gb[:, s, 0:8], True, True, [b_mui, b_gb], [bg_])
            cp(c, "dve", cmb[:, s, 0:8], psg[:, 0:8], [bg_], [b_cmb])
            cp(c, "pool", cmb[:, s, 8:16], gb[:, s, 8:16], [b_gb], [b_cmb])
            pst_, bt_ = c.pp.get()
            tr(c, pst_[0:16, 0:128], cmb[:, s, :], c.ident_f[:], [b_cmb, c.b_ident], [bt_])
            cp(c, "dve", gbT[0:16, s, :], pst_[0:16, 0:128], [bt_], [b_gbT])
        bd = Buf("gbT_s%d" % it)
        dma(c, c.gbT_s[2 * it:2 * it + 2].rearrange("s h i -> h s i"), gbT[0:16, :, :], [b_gbT], [bd], "st_gbT")
        gbT_b.append(bd)
        bd = Buf("gate_s%d" % it)
        dma(c, c.gate_s[t0:t0 + TT, :].rearrange("(s p) n -> p s n", p=128), gate_sb, [b_gate], [bd], "st_gate")
        gate_b.append(bd)

        bd = Buf("kvtok_s%d" % it)
        dma(c, c.kvtok_s[t0:t0 + TT, :].rearrange("(s p) n -> p s n", p=128), tok, [b_tok], [bd], "st_tok")
        kvtok_b.append(bd)
    c.p2 = dict(gq=gq_b, gk=gk_b, kvtok=kvtok_b, gb=gb_b, gate=gate_b, qT=qT_b, kn=kn_b, kpe=kpe_b, v=v_b, gbT=gbT_b)
    for l in c.p2.values():
        c.dbg_final += l
    barrier(c, allb, R.bufs + R2.bufs)
    return


def phase3(c, pump=None):
    S, T, NT = c.S, c.T, c.NT
    R = c.reg_d
    R.reset()
    allb = []

    def alloc(n, dt):
        ap, _ = R.alloc(n, dt)
        b = Buf("p3_%d" % len(allb))
        allb.append(b)
        return ap, b

    c.pumped = 0
    kT = [alloc(T, BF16) for _ in range(2)]
    V = [alloc(NT * 65, BF16) for _ in range(2)]
    qb = RPool([alloc(512, BF16) for _ in range(2)])
    PT = RPool([alloc(512, BF16) for _ in range(4)])
    oT = RPool([alloc(512, F32) for _ in range(2)])
    osb = RPool([alloc(256, F32) for _ in range(2)])
    rden, b_rden = alloc(4, F32)
    tri_f, b_trif = alloc(128, F32)
    tri, b_tri = alloc(128, BF16)
    barrier(c, R.bufs, allb)
    dma(c, tri_f, c.masks_d[:, 5, :], [], [b_trif], "trif")
    cp(c, "dve", tri, tri_f, [b_trif], [b_tri])
    pp_s = RPool([(c.ps[i], c.b_ps[i]) for i in range(4)])
    pp_o = RPool([(c.ps[i], c.b_ps[i]) for i in (4, 5)])
    pp_t = RPool([(c.ps[i], c.b_ps[i]) for i in (6, 7)])
    scale = float(96 ** -0.5)
    NQB = T // 512
    out_bufs = []
    p2 = c.p2
    def load_kv(h):
        kt, bkt = kT[h % 2]
        v, bv = V[h % 2]
        v3 = v.rearrange("p (j e) -> p j e", e=65)
        dma(c, kt[0:64, :], c.knope_s[h], p2["kn"], [bkt], "kt%d" % (h % 2))
        dma(c, kt[64:96, :], c.kpe_s, p2["kpe"], [bkt], "kt%d" % (h % 2))
        dma(c, v3[:, :, 0:64], c.v_s[:, :, h * 64:(h + 1) * 64], p2["v"], [bv], "v%d" % (h % 2))
        S.emit("pool", (lambda e, v3=v3: e.memset(v3[:, :, 64:65], 1.0)), writes=[bv])

    def load_q(h, I):
        q, bq = qb.get()
        dma(c, q[0:96, :], c.qT_s[h][:, I * 512:(I + 1) * 512], p2["qT"], [bq], "qb%d" % ((qb.i - 1) % 2))
        return q, bq

    items = [(h, I) for h in range(NH) for I in range(NQB)]
    load_kv(0)
    qn = load_q(*items[0])
    for ii, (h, I) in enumerate(items):
        kt, bkt = kT[h % 2]
        v, bv = V[h % 2]
        v3 = v.rearrange("p (j e) -> p j e", e=65)
        if I == 0 and h + 1 < NH:
            load_kv(h + 1)
        if True:
            if pump is not None and c.pumped < KC:
                next(pump, None)
                c.pumped += 1
            elif c.wo_next < KC:
                load_wo_row(c)
            q, bq = qn
            if ii + 1 < len(items):
                qn = load_q(*items[ii + 1])
            nj = 4 * I + 4
            psO, bO = pp_o.get()
            sres = {}

            def emit_s(j):
                ps, bp = pp_s.get()
                c0 = max(0, j - 4 * I) * 128
                w = 512 - c0
                mm(c, ps[:, 0:w], kt[0:96, j * 128:(j + 1) * 128], q[0:96, c0:512], True, True, [bkt, bq], [bp])
                sres[j] = (ps, bp, c0, w)

            def emit_pv(j):
                ps, bp, c0, w = sres.pop(j)
                pt, bpt = PT.get()
                act(c, pt[:, 0:w], ps[:, 0:w], AF.Exp, [bp], [bpt], scale=scale)
                if j >= 4 * I:
                    tt(c, "pool", pt[:, 0:128], pt[:, 0:128], tri, ALU.mult, [bpt, b_tri], [bpt])
                mm(c, psO[0:65, c0:512], v3[:, j, 0:65], pt[:, 0:w], j == 0, j == nj - 1, [bv, bpt], [bO])

            for j0 in range(min(3, nj)):
                emit_s(j0)
            for j in range(nj):
                if j + 3 < nj:
                    emit_s(j + 3)
                emit_pv(j)
            o_f, bof = oT.get()
            cp(c, "dve", o_f[0:65, :], psO[0:65, :], [bO], [bof])
            psT, bT = pp_t.get()
            psT4 = psT[:].rearrange("p (q e) -> p q e", q=4)
            for qq in range(4):
                tr(c, psT4[:, qq, 0:65], o_f[0:65, qq * 128:(qq + 1) * 128], c.ident_f[0:65, 0:65], [bof, c.b_ident], [bT])
            S.emit("dve", (lambda e, psT4=psT4: e.reciprocal(out=rden.unsqueeze(2), in_=psT4[:, :, 64:65])),
                   reads=[bT], writes=[b_rden])
            o_sb, bo = osb.get()
            o3 = o_sb.rearrange("p (q e) -> p q e", q=4)
            tt(c, "dve", o3, psT4[:, :, 0:64], rden.unsqueeze(2).to_broadcast([128, 4, 64]), ALU.mult, [bT, b_rden], [bo])
            bd = Buf("mla_s%d_%d" % (h, I))
            dma(c, c.mla_s[I * 512:(I + 1) * 512, h * 64:(h + 1) * 64].rearrange("(q p) e -> p q e", p=128), o3,
                [bo], [bd], "st_o%d" % ((osb.i - 1) % 2))
            out_bufs.append(bd)
    c.mla_bufs = out_bufs
    c.dbg_final += out_bufs
    barrier(c, allb, R.bufs)


def phase4(c):
    S_, T, NT = c.S, c.T, c.NT
    RU, RD = c.reg_u, c.reg_d
    RU.reset()
    RD.reset()
    allb = []
    state = {"r": RU}

    def alloc(n, dt):
        esz = 4 if dt in (F32, I32) else 2
        r = state["r"]
        if (r.off + 31) // 32 * 32 + n * esz > r.nbytes:
            state["r"] = r = RD
        ap, _ = r.alloc(n, dt)
        b = Buf("p4_%d" % len(allb))
        allb.append(b)
        return ap, b

    def a3(n_outer, n_inner, dt):
        ap, b = alloc(n_outer * n_inner, dt)
        return ap.rearrange("p (h i) -> p h i", h=n_outer), b

    def nb(name):
        b = Buf(name)
        allb.append(b)
        return b

    QKp = RPool([alloc(8 * 2 * 128, BF16) for _ in range(2)])
    KVp = RPool([alloc(1024, BF16) for _ in range(2)])
    gbp = RPool([alloc(16, F32) for _ in range(2)])
    gatep = RPool([alloc(512, F32) for _ in range(2)])
    Gb, b_Gb = a3(16, 128, F32)
    MUi, b_MUi = alloc(128, F32)
    NMUi, b_NMUi = alloc(128, F32)
    MUs, b_MUs = alloc(128, F32)
    NMLs, b_NMLs = alloc(128, F32)
    BLK, b_BLK = alloc(128, F32)
    ones, b_ones = alloc(128, F32)
    gcgl, b_gcgl = alloc(16, F32)
    sm, b_sm = alloc(32, F32)
    EU, _ = a3(8, 128, F32)
    EL, _ = a3(8, 128, F32)
    ExpG, b_ExpG = a3(8, 128, F32)
    tmpA, b_tmpA = a3(8, 128, F32)
    tmpB, b_tmpB = a3(8, 128, F32)
    Dg, b_Dg = ExpG, b_ExpG
    Db, b_Db = tmpB, b_tmpB
    BM, b_BM = tmpA, b_tmpA
    Nm, _ = a3(8, 128, F32)
    Lm, _ = a3(8, 128, F32)
    Pm, Rm = EU, EL
    bA2 = [nb("p4tA%d" % i) for i in range(2)]
    bB2 = [nb("p4tB%d" % i) for i in range(2)]
    bX2 = [nb("p4X%d" % i) for i in range(2)]
    bN = [nb("p4N%d" % i) for i in range(2)]
    bL = [nb("p4L%d" % i) for i in range(2)]
    bP = [nb("p4P%d" % i) for i in range(2)]
    bR = [nb("p4R%d" % i) for i in range(2)]
    Pb, b_Pb = a3(8, 128, BF16)
    Kbg, b_Kbg = alloc(512, BF16)
    Vb, b_Vb = alloc(512, BF16)
    HO = []
    for i in range(2):
        h = Ctx()
        h.attnT, h.b_attnT = a3(8, 128, BF16)
        h.wT, h.b_wT = a3(8, 128, BF16)
        h.QdT, h.b_QdT = a3(8, 128, BF16)
        h.Kd, h.b_Kd = alloc(512, BF16)
        h.u, h.b_u = alloc(512, F32)
        h.glS, h.b_glS = alloc(16, F32)
        HO.append(h)
    vnew, b_vnew = alloc(512, BF16)
    o_sb, b_o = alloc(512, F32)
    rs, b_rs = alloc(8, F32)
    Sf, b_Sf = alloc(512, F32)
    Sb, b_Sb = alloc(512, BF16)
    Stmp, b_Stmp = alloc(512, F32)
    barrier(c, RU.bufs + RD.bufs, allb)
    for i, (ap, b) in enumerate(((MUi, b_MUi), (NMUi, b_NMUi), (MUs, b_MUs), (NMLs, b_NMLs), (BLK, b_BLK))):
        dma(c, ap, c.masks_d[:, i, :], [], [b], "p4m%d" % i)
    S_.emit("pool", (lambda e: e.memset(ones, 1.0)), writes=[b_ones])
    S_.emit("pool", (lambda e: e.memset(Sf, 0.0)), writes=[b_Sf])
    S_.emit("pool", (lambda e: e.memset(Sb, 0.0)), writes=[b_Sb])
    identb = c.ident_f[:].unsqueeze(1)
    p2 = c.p2
    out_bufs = []
    c.gdn_bufs = out_bufs
    H4 = lambda ap, hh: ap[:, 4 * hh:4 * hh + 4, :]
    bc4 = lambda col_ap: col_ap.unsqueeze(2).to_broadcast([128, 4, 128])
    m4 = lambda m: m.unsqueeze(1).to_broadcast([128, 4, 128])
    bc8 = lambda col: col.unsqueeze(2).to_broadcast([128, 8, 64])
    v4 = lambda ps: ps[:].rearrange("p (h i) -> p h i", h=4)
    ppP = RPool([(c.ps[i], c.b_ps[i]) for i in range(5)])
    ppS = RPool([(c.ps[i], c.b_ps[i]) for i in (5, 6, 7)])

    def loads(t):
        t0 = t * 128
        L = Ctx()
        L.QK, L.b_QK = QKp.get()
        L.QK5 = L.QK.rearrange("p (c a k t) -> p c a k t", c=4, a=2, k=2)
        key = "p4qk%d" % ((QKp.i - 1) % 2)
        for hp in range(2):
            dma(c, L.QK5[0:64, :, hp, 0, :], c.gk_s[:, hp * 64:(hp + 1) * 64, t0:t0 + 128].rearrange("c d t -> d c t"),
                p2["gk"], [L.b_QK], key)
            dma(c, L.QK5[0:64, :, hp, 1, :], c.gq_s[:, hp * 64:(hp + 1) * 64, t0:t0 + 128].rearrange("c d t -> d c t"),
                p2["gq"], [L.b_QK], key)
        L.KV, L.b_KV = KVp.get()
        dma(c, L.KV, c.kvtok_s[t0:t0 + 128, :], p2["kvtok"], [L.b_KV], "p4kv%d" % ((KVp.i - 1) % 2))
        L.gbt, L.b_gbt = gbp.get()
        dma(c, L.gbt, c.gb_s[t0:t0 + 128, :], p2["gb"], [L.b_gbt], "p4gb%d" % ((gbp.i - 1) % 2))
        dma(c, Gb, c.gbT_s[t].partition_broadcast(128), p2["gbT"], [b_Gb], "p4Gb")
        L.gatet, L.b_gatet = gatep.get()
        dma(c, L.gatet, c.gate_s[t0:t0 + 128, :], p2["gate"], [L.b_gatet], "p4gate%d" % ((gatep.i - 1) % 2))
        return L

    def prep_gen(t, L, ho):
        QK5, b_QK, KV, b_KV, gbt, b_gbt = L.QK5, L.b_QK, L.KV, L.b_KV, L.gbt, L.b_gbt
        beta = gbt[:, 8:16]
        ps, bp = ppP.get()
        mm(c, ps[:, 0:8], MUi, gbt[:, 0:8], True, True, [b_MUi, b_gbt], [bp])
        mm(c, ps[:, 8:16], BLK, gbt[:, 0:8], True, True, [b_BLK, b_gbt], [bp])
        cp(c, "dve", gcgl, ps[:, 0:16], [bp], [b_gcgl])
        gc = gcgl[:, 0:8]
        act(c, sm[:, 0:8], gc, AF.Exp, [b_gcgl], [b_sm])
        tt(c, "dve", sm[:, 24:32], gcgl[:, 8:16], gc, ALU.subtract, [b_gcgl], [b_sm])
        act(c, sm[:, 8:16], sm[:, 24:32], AF.Exp, [b_sm], [b_sm])
        tt(c, "dve", sm[:, 16:24], sm[:, 0:8], beta, ALU.mult, [b_sm, b_gbt], [b_sm])
        K3 = KV[:, 0:512].rearrange("p (h e) -> p h e", h=8)
        V3 = KV[:, 512:1024].rearrange("p (h e) -> p h e", h=8)
        tt(c, "pool", Kbg.rearrange("p (h e) -> p h e", h=8), K3, bc8(sm[:, 16:24]), ALU.mult, [b_KV, b_sm], [b_Kbg])
        tt(c, "pool", ho.Kd.rearrange("p (h e) -> p h e", h=8), K3, bc8(sm[:, 8:16]), ALU.mult, [b_KV, b_sm], [ho.b_Kd])
        tt(c, "pool", Vb.rearrange("p (h e) -> p h e", h=8), V3, bc8(beta), ALU.mult, [b_KV, b_gbt], [b_Vb])
        yield
        G = [(Gb[:, 4 * hh:4 * hh + 4, :], b_Gb) for hh in range(2)]
        Bb = [(Gb[:, 8 + 4 * hh:12 + 4 * hh, :], b_Gb) for hh in range(2)]
        def e_ops(hh):
            g3, bg = G[hh]
            b3, bb = Bb[hh]
            return [
                lambda: tt(c, "dve", H4(tmpA, hh), g3, bc4(gc[:, 4 * hh:4 * hh + 4]), ALU.subtract, [bg, b_gcgl], [bA2[hh]]),
                lambda: act(c, H4(ExpG, hh), g3, AF.Exp, [bg], [bX2[hh]]),
                lambda: tt(c, "dve", H4(tmpB, hh), H4(tmpA, hh), m4(NMUi), ALU.min, [bA2[hh], b_NMUi], [bB2[hh]]),
                lambda: act(c, H4(EU, hh), H4(tmpB, hh), AF.Exp, [bB2[hh]], [bP[hh]]),
                lambda: stt(c, H4(tmpB, hh), H4(tmpA, hh), -1.0, m4(NMLs), ALU.mult, ALU.min, [bA2[hh], b_NMLs], [bB2[hh]]),
                lambda: act(c, H4(EL, hh), H4(tmpB, hh), AF.Exp, [bB2[hh]], [bR[hh]]),
                lambda: tt(c, "pool", H4(BM, hh), b3, m4(MUs), ALU.mult, [bb, b_MUs], [bA2[hh]]),
            ]

        for fa, fb in zip(e_ops(0), e_ops(1)):
            fa()
            fb()
        yield
        tt(c, "dve", ho.QdT[0:64].rearrange("p (c two) i -> p c two i", two=2), QK5[0:64, :, :, 1, :],
           ExpG[0:64].rearrange("p (c two) i -> p c two i", two=2), ALU.mult, [b_QK] + bX2, [ho.b_QdT])
        for ch in range(2):
            cp(c, "pool", ho.glS[0:64, ch * 8:(ch + 1) * 8], ExpG[0:64, :, ch * 64 + 63], bX2, [ho.b_glS])
        KQ = []
        for hh in range(2):
            psK, bK = ppP.get()
            psQ, bQ = ppP.get()
            for hl in range(4):
                h = 4 * hh + hl
                pr, hp = h // 2, h % 2
                kt = QK5[0:64, pr, hp, 0, :]
                qt = QK5[0:64, pr, hp, 1, :]
                mm(c, psK[:, hl * 128:(hl + 1) * 128], kt, kt, True, True, [b_QK], [bK])
                mm(c, psQ[:, hl * 128:(hl + 1) * 128], kt, qt, True, True, [b_QK], [bQ])
            KQ.append((v4(psK), bK, v4(psQ), bQ))

        def k_ops(hh):
            k3, bK, q3, bQ = KQ[hh]
            return [
                lambda: tt(c, "dve", H4(Nm, hh), k3, H4(EU, hh), ALU.mult, [bK, bP[hh]], [bN[hh]]),
                lambda: tt(c, "dve", H4(Lm, hh), k3, H4(EL, hh), ALU.mult, [bK, bR[hh]], [bL[hh]]),
                lambda: tt(c, "pool", H4(Nm, hh), H4(Nm, hh), H4(BM, hh), ALU.mult, [bN[hh], bA2[hh]], [bN[hh]]),
                lambda: tt(c, "dve", H4(ho.attnT, hh), q3, H4(EU, hh), ALU.mult, [bQ, bP[hh]], [ho.b_attnT]),
                lambda: tt(c, "pool", H4(Lm, hh), H4(Lm, hh), bc4(beta[:, 4 * hh:4 * hh + 4]), ALU.mult, [bL[hh], b_gbt], [bL[hh]]),
                lambda: tt(c, "pool", H4(Pm, hh), identb.to_broadcast([128, 4, 128]), H4(Nm, hh), ALU.subtract,
                           [c.b_ident, bN[hh]], [bP[hh]]),
                lambda: tt(c, "pool", H4(Rm, hh), identb.to_broadcast([128, 4, 128]), H4(Lm, hh), ALU.subtract,
                           [c.b_ident, bL[hh]], [bR[hh]]),
            ]

        for fa, fb in zip(k_ops(0), k_ops(1)):
            fa()
            fb()
        yield
        for k in range(1, 6):
            last = (k == 5)
            sq = []
            for hh in range(2):
                psN, bpN = ppP.get()
                for hl in range(4):
                    h = 4 * hh + hl
                    mm(c, psN[:, hl * 128:(hl + 1) * 128], Lm[:, h, :], Nm[:, h, :], True, True, [bL[hh], bN[hh]], [bpN])
                sq.append((psN, bpN))
            for hh in range(2):
                psN, bpN = sq[hh]
                cp(c, "act", H4(Nm, hh), v4(psN), [bpN], [bN[hh]])
            yield
            tl, up = [], []
            for hh in range(2):
                if not last:
                    psL, bpL = ppP.get()
                    for hl in range(4):
                        h = 4 * hh + hl
                        tr(c, psL[:, hl * 128:(hl + 1) * 128], Nm[:, h, :], c.ident_f[:], [bN[hh], c.b_ident], [bpL])
                    tl.append((psL, bpL))
                psP, bpP = ppP.get()
                for hl in range(4):
                    h = 4 * hh + hl
                    mm(c, psP[:, hl * 128:(hl + 1) * 128], Rm[:, h, :], Nm[:, h, :], True, True, [bR[hh], bN[hh]], [bpP])
                up.append((psP, bpP))
            for hh in range(2):
                if not last:
                    psL, bpL = tl[hh]
                    cp(c, "act", H4(Lm, hh), v4(psL), [bpL], [bL[hh]])
                psP, bpP = up[hh]
                tt(c, "dve", H4(Pm, hh), H4(Pm, hh), v4(psP), ALU.add, [bP[hh], bpP], [bP[hh]])
            yield
            if not last:
                tr_ = []
                for hh in range(2):
                    psR, bpR = ppP.get()
                    for hl in range(4):
                        h = 4 * hh + hl
                        tr(c, psR[:, hl * 128:(hl + 1) * 128], Pm[:, h, :], c.ident_f[:], [bP[hh], c.b_ident], [bpR])
                    tr_.append((psR, bpR))
                for hh in range(2):
                    psR, bpR = tr_[hh]
                    cp(c, "act", H4(Rm, hh), v4(psR), [bpR], [bR[hh]])
                yield
        for hh in range(2):
            cp(c, "act", H4(Pb, hh), H4(Pm, hh), [bP[hh]], [b_Pb])
        psU, bU = ppP.get()
        for hh in range(2):
            psW, bW = ppP.get()
            for hl in range(4):
                h = 4 * hh + hl
                mm(c, psW[0:64, hl * 128:(hl + 1) * 128], Kbg[:, h * 64:(h + 1) * 64], Pb[:, h, :], True, True,
                   [b_Kbg, b_Pb], [bW])
            cp(c, "act", ho.wT[0:64, 4 * hh:4 * hh + 4, :], psW[0:64, :].rearrange("p (c i) -> p c i", c=4), [bW], [ho.b_wT])
        for h in range(8):
            mm(c, psU[:, h * 64:(h + 1) * 64], Pb[:, h, :], Vb[:, h * 64:(h + 1) * 64], True, True, [b_Pb, b_Vb], [bU])
        cp(c, "act", ho.u, psU[:, 0:512], [bU], [ho.b_u])
        yield

    def scan_gen(t, L, ho):
        t0 = t * 128
        attnT, wT, QdT, Kd, u, glS = ho.attnT, ho.wT, ho.QdT, ho.Kd, ho.u, ho.glS
        for ch in range(2):
            rows = slice(ch * 64, (ch + 1) * 64)
            cols = slice(ch * 64, (ch + 1) * 64)
            psA, bA = ppS.get()
            psB, bB = ppS.get()
            for h in range(8):
                sb_h = Sb[0:64, h * 64:(h + 1) * 64]
                mm(c, psA[rows, h * 64:(h + 1) * 64], wT[0:64, h, cols], sb_h, True, True, [ho.b_wT, b_Sb], [bA])
            for h in range(8):
                sb_h = Sb[0:64, h * 64:(h + 1) * 64]
                mm(c, psB[rows, h * 64:(h + 1) * 64], QdT[0:64, h, cols], sb_h, True, True, [ho.b_QdT, b_Sb], [bB])
            tt(c, "dve", vnew[rows, :], u[rows, :], psA[rows, 0:512], ALU.subtract, [ho.b_u, bA], [b_vnew])
            cp(c, "act", o_sb[rows, :], psB[rows, 0:512], [bB], [b_o])
            tt(c, "pool", Stmp[0:64].rearrange("p (c e) -> p c e", c=8), Sf[0:64].rearrange("p (c e) -> p c e", c=8),
               glS[0:64, ch * 8:(ch + 1) * 8].unsqueeze(2).to_broadcast([64, 8, 64]), ALU.mult, [b_Sf, ho.b_glS], [b_Stmp])
            yield
            psD, bD = ppS.get()
            for h in range(8):
                vn_h = vnew[rows, h * 64:(h + 1) * 64]
                mm(c, psD[0:64, h * 64:(h + 1) * 64], Kd[rows, h * 64:(h + 1) * 64], vn_h, True, True, [ho.b_Kd, b_vnew], [bD])
            tt(c, "dve", Sf[0:64], Stmp[0:64], psD[0:64, 0:512], ALU.add, [b_Stmp, bD], [b_Sf])
            cp(c, "act", Sb[0:64], Sf[0:64], [b_Sf], [b_Sb])
            psC, bC = ppS.get()
            for h in range(8):
                vn_h = vnew[rows, h * 64:(h + 1) * 64]
                mm(c, psC[rows, h * 64:(h + 1) * 64], attnT[rows, h, cols], vn_h, True, True, [ho.b_attnT, b_vnew], [bC])
            tt(c, "dve", o_sb[rows, :], o_sb[rows, :], psC[rows, 0:512], ALU.add, [b_o, bC], [b_o])
            yield
        osq, b_osq = Stmp, b_Stmp
        tt(c, "pool", osq, o_sb, o_sb, ALU.mult, [b_o], [b_osq])
        S_.emit("dve", (lambda e: e.tensor_reduce(out=rs, in_=osq.rearrange("p (h e) -> p h e", h=8), axis=AX.X, op=ALU.add)),
                reads=[b_osq], writes=[b_rs])
        rstd_from_ss(c, rs, rs, 64, 1.0, b_rs)
        yield
        o3 = o_sb.rearrange("p (h e) -> p h e", h=8)
        tt(c, "dve", o3, o3, bc8(rs), ALU.mult, [b_o, b_rs], [b_o])
        tt(c, "pool", o_sb, o_sb, L.gatet, ALU.mult, [b_o, L.b_gatet], [b_o])
        bd = Buf("gdn_s%d" % t)
        dma(c, c.gdn_s[t0:t0 + 128, :], o_sb, [b_o], [bd], "st_gdn")
        out_bufs.append(bd)
        yield

    Ls = {0: loads(0)}
    stop = getattr(c, "p4_stop", 99)
    for i, _ in enumerate(prep_gen(0, Ls[0], HO[0])):
        if i + 1 >= stop:
            break
    for t in range(NT if stop >= 50 else 0):
        gp = iter(())
        if t + 1 < NT:
            Ls[t + 1] = loads(t + 1)
            gp = prep_gen(t + 1, Ls[t + 1], HO[(t + 1) % 2])
        gs = scan_gen(t, Ls[t], HO[t % 2])
        if getattr(c, "p4_serial", False):
            for _ in gs:
                pass
            for _ in gp:
                pass
        pa = sa = not getattr(c, "p4_serial", False)
        while pa or sa:
            if pa:
                pa = next(gp, "end") != "end"
            if pa:
                pa = next(gp, "end") != "end"
            if sa:
                sa = next(gs, "end") != "end"
        Ls.pop(t)
        if getattr(c, "op_on", False) and t % 2 == 1:
            it = t // 2
            if it >= 1:
                outproj_compute(c, it - 1)
            outproj_loads(c, it)
    if getattr(c, "op_on", False):
        outproj_compute(c, NT // 2 - 1)
    c.gdn_bufs = out_bufs
    c.dbg_final += out_bufs
    barrier(c, allb, RU.bufs + RD.bufs)


def phase5(c, pump_unused):
    S, T = c.S, c.T
    for _ in pump_unused:
        pass
    while c.wo_next < KC:
        load_wo_row(c)
    R = c.reg_u
    R.reset()
    allb = []

    def alloc(n, dt):
        ap, _ = R.alloc(n, dt)
        b = Buf("p5_%d" % len(allb))
        allb.append(b)
        return ap, b

    mop = RPool([alloc(1024, F32) for _ in range(2)])
    gop = RPool([alloc(1024, F32) for _ in range(2)])
    barrier(c, R.bufs, allb)
    load_gbc(c, 1)
    wd_src = c.w["ffn2_d"]
    wd_pieces = [(f, c0, min(D, c0 + PIECE)) for f in range(NF) for c0 in range(0, D, PIECE)]
    wd_state = {"i": 0, "pend": []}

    def wd_issue(k):
        for _ in range(k):
            if wd_state["i"] >= len(wd_pieces):
                return
            f, c0, c1 = wd_pieces[wd_state["i"]]
            wd_state["i"] += 1
            si = c.stg_i % NSTG
            c.stg_i += 1
            stg, bst = c.stg[si], c.b_stg[si]
            dma(c, stg[:, 0:c1 - c0], wd_src[f * 128:(f + 1) * 128, c0:c1], [], [bst], "stg%d" % si)
            wd_state["pend"].append((f, c0, c1, stg, bst))

    def wd_cast():
        for f, c0, c1, stg, bst in wd_state["pend"]:
            cast_scaled(c, c.wd[:, f, c0:c1], stg[:, 0:c1 - c0], None, [bst], [c.b_wd[f]])
        wd_state["pend"] = []
    n = T // TT

    def loads(it):
        t0 = it * TT
        x = load_tile(c, c.x1_s, it, [c.x1_bufs[it]])
        mo, bmo = mop.get()
        go, bgo = gop.get()
        k = (mop.i - 1) % 2
        mo3 = mo.rearrange("p (s n) -> p s n", s=2)
        go3 = go.rearrange("p (s n) -> p s n", s=2)
        dma(c, mo3, c.mla_s[t0:t0 + TT, :].rearrange("(s p) n -> p s n", p=128), c.mla_bufs, [bmo], "p5mo%d" % k)
        dma(c, go3, c.gdn_s[t0:t0 + TT, :].rearrange("(s p) n -> p s n", p=128), c.gdn_bufs, [bgo], "p5go%d" % k)
        return x, (mo3, bmo), (go3, bgo)

    x2_bufs = []

    def pre(ld):
        (xt, bxt, xi), (mo3, bmo), (go3, bgo) = ld
        for s in range(2):
            bst = c.b_st[14 + s]
            ssap = c.st[:, 14 + s:15 + s]
            act(c, c.junk[:, 0:512], mo3[:, s, :], AF.Square, [bmo], [c.b_junk, bst], accum_out=ssap)
            rstd_from_ss(c, ssap, ssap, 512, 1.0, bst)
            ts(c, "dve", c.xn[:, s, 0:512], mo3[:, s, :], ssap, None, ALU.mult, None, [bmo, bst], [c.b_xn[s]])
            cp(c, "pool", c.xn[:, s, 512:1024], go3[:, s, :], [bgo], [c.b_xn[s]])

    cur = loads(0)
    pre(cur)
    c.pump = None
    for it in range(n):
        (xt, bxt, xi), _, _ = cur
        nxt = loads(it + 1) if it + 1 < n else None
        wd_issue(NSTG)
        save = c.pp
        c.pp = c.pp_gu
        for s in range(2):
            xn_to_T(c, s)
        c.pp = save
        if nxt is not None:
            pre(nxt)
        yb = [[None, None], [None, None]]
        for s in range(2):
            for hh in range(2):
                ps, bps = c.pp_y[s * 2 + hh]
                yb[s][hh] = (ps, bps)
                for kc in range(KC):
                    mm(c, ps[:], c.xnT[:, kc, s * 128:(s + 1) * 128], c.wo[:, kc, hh * 512:(hh + 1) * 512],
                       kc == 0, kc == KC - 1, [c.b_xnT, c.b_wo[kc]], [bps])
        c.post_scale = 1.0
        post_residual(c, xt, bxt, yb)
        x2_bufs.append(store_tile(c, xt, bxt, xi, c.x2_s, it, "x2"))
        wd_cast()
        cur = nxt
    while wd_state["i"] < len(wd_pieces):
        wd_issue(NSTG)
        wd_cast()
    barrier(c, allb, R.bufs)
    load_gbc(c, 2)
    c.pump = ffn_load_weights(c, "ffn2", COL_G_FFN2, parts=("u",))
    next(c.pump)
    next(c.pump)
    S.final_bufs += ffn_phase(c, c.x2_s, x2_bufs, c.out, "out")
    for _ in c.pump:
        pass
    c.pump = None
def host_consts(inp):
    f = lambda k: np.asarray(inp[k], np.float32).reshape(-1)
    cs = np.zeros((128, NCONST), np.float32)
    for col, nm in ((COL_G_FFN1, "ffn1_pre_g"), (COL_G_MIX, "mix_pre_g"), (COL_G_FFN2, "ffn2_pre_g")):
        cs[:, col:col + KC] = f(nm).reshape(KC, 128).T
    cs[:, COL_G_Q:COL_G_Q + 2] = f("mla_q_norm_g").reshape(2, 128).T
    cs[:, COL_G_KV] = f("mla_kv_norm_g")
    cw = np.asarray(inp["gdn_conv_w"], np.float32).reshape(4, 1536)
    cs[:, COL_CONV:COL_CONV + 48] = cw.reshape(4, 12, 128).transpose(2, 1, 0).reshape(128, 48)
    half = 16
    freqs = (10000.0 ** (-np.arange(half, dtype=np.float32) / np.float32(half))).astype(np.float32)
    p = np.arange(128)
    cs[:, COL_FREQ] = freqs[(p % 32) % 16]
    cs[:, COL_SIGN] = np.where((p % 32) < 16, -1.0, 1.0)
    cs[:, COL_G_OUT:COL_G_OUT + 4] = f("mla_out_g").reshape(4, 128).T
    cs[:, COL_G_OUT + 4:COL_G_OUT + 8] = np.tile(f("gdn_norm_g"), 2)[:, None]
    cs[:, COL_DTB:COL_DTB + 8] = f("gdn_dt_bias")[None, :]
    cs[:, COL_ALOG:COL_ALOG + 8] = f("gdn_a_log")[None, :]
    return cs


def host_masks():
    m = np.zeros((128, 6, 128), np.float32)
    j = np.arange(128)[:, None]
    i = np.arange(128)[None, :]
    same = (j // 64) == (i // 64)
    m[:, 0] = (same & (j <= i))
    m[:, 1] = np.where(same & (j <= i), 0.0, -1.0e4)
    m[:, 2] = (same & (j < i))
    m[:, 3] = np.where(same & (i < j), 0.0, -1.0e4)
    m[:, 4] = same
    m[:, 5] = (j <= i)
    return m


def host_weights(inp):
    w_in = np.asarray(inp["w_in"], np.float32).reshape(D, 2480)
    ext = np.zeros((D, NWIN), np.float32)
    ext[:, W_CQ0:W_CQ0 + 256] = w_in[:, 0:256]
    ext[:, W_CKV0:W_CKV0 + 128] = w_in[:, 256:384]
    kpe = w_in[:, 384:416]
    ext[:, W_KPE0 + 64:W_KPE0 + 96] = kpe
    ext[:, W_KPESW0 + 64:W_KPESW0 + 96] = np.concatenate([kpe[:, 16:32], kpe[:, 0:16]], axis=1)
    ext[:, W_QKV0:W_QKV0 + 1536] = w_in[:, 416:1952]
    ext[:, W_AB0:W_AB0 + 16] = w_in[:, 1952:1968]
    ext[:, W_GATE0:W_GATE0 + 512] = w_in[:, 1968:2480]
    w_uq = np.asarray(inp["mla_w_uq"], np.float32).reshape(256, 768)
    uq = np.zeros((256, 1536), np.float32)
    for h in range(NH):
        blk = w_uq[:, h * 96:(h + 1) * 96]
        uq[:, h * 192:h * 192 + 96] = blk
        uq[:, h * 192 + 96 + 64:h * 192 + 192] = np.concatenate([blk[:, 80:96], blk[:, 64:80]], axis=1)
    w_ukv = np.asarray(inp["mla_w_ukv"], np.float32).reshape(128, 1024).reshape(128, NH, 128)
    ukv = np.concatenate([w_ukv[:, :, 0:64].reshape(128, 512), w_ukv[:, :, 64:128].reshape(128, 512)], axis=1)
    return ext, uq, np.ascontiguousarray(ukv)


def host_inputs(inputs, b, T):
    f = lambda k: np.ascontiguousarray(np.asarray(inputs[k], np.float32)[0])
    ext, uq, ukv = host_weights({k: np.asarray(v)[0] for k, v in inputs.items() if k in ("w_in", "mla_w_uq", "mla_w_ukv")})
    cin = {k: np.asarray(v)[0] for k, v in inputs.items() if k not in ("x", "positions")}
    m = {
        "x": np.ascontiguousarray(np.asarray(inputs["x"], np.float32)[b][:T]),
        "pos": np.ascontiguousarray(np.asarray(inputs["positions"], np.int32)[b][None, :T]),
        "ident": np.eye(128, dtype=np.float32),
        "masks": host_masks(),
        "consts": host_consts(cin),
        "gpost": np.stack([f("ffn1_post_g"), f("mix_post_g"), f("ffn2_post_g")]),
        "w_in_ext": ext, "w_uq_ext": uq, "w_ukv_ext": ukv, "w_out": f("w_out"),
    }
    for nm in ("ffn1", "ffn2"):
        m[nm + "_wg"] = f(nm + "_w_gate")
        m[nm + "_wu"] = f(nm + "_w_up")
        m[nm + "_wd"] = f(nm + "_w_down")
    return m


_T_FULL = 4096
_NCORES = 8


def kernel(**inputs):
    nc = build_program(_T_FULL)
    in_maps = [host_inputs(inputs, b, _T_FULL) for b in range(_NCORES)]
    for m in in_maps[1:]:
        for k in m:
            if k not in ("x", "pos"):
                m[k] = in_maps[0][k]
    res = run_bass_kernel_spmd(nc, in_maps, core_ids=list(range(_NCORES)))
    out = np.stack([np.asarray(r["out"], np.float32) for r in res.results], axis=0)
    return out
```
